# Optimizing a Trainium2 kernel written in Bass

```python
import jax, jax.numpy as jnp
from jax import lax
import numpy as np

D_MODEL = 1024
BATCH = 8
SEQ = 2048
DEPTH = 2
DEC_BATCH = 128
DEC_SEQ = 1
PAST_LEN = 16384
PAGE_SIZE = 128

N_RET_LAYERS = (DEPTH + 1) // 2
N_CONV_LAYERS = DEPTH // 2
RET_HEADS = 4
RET_DK = D_MODEL // 8
RET_DV = D_MODEL // 4
RET_CHUNK = 128
ROPE_BASE = 10000.0
SGU_GROUPS = 4
SGU_GD = D_MODEL // 4
SGU_CHUNK = 128
CONV_WIDTH = 31
D_CONV = D_MODEL
D_FF = 4 * D_MODEL
RMS_EPS = 1e-6
LN_EPS = 1e-5

Q_W = RET_HEADS * RET_DK
V_W = RET_HEADS * RET_DV
U_W = SGU_GROUPS * SGU_GD
SPLIT_IDX = (Q_W, 2 * Q_W, 2 * Q_W + V_W, 2 * Q_W + 2 * V_W, 2 * Q_W + 2 * V_W + U_W)
D_IN = 2 * Q_W + 2 * V_W + 2 * U_W
D_MIX_OUT = V_W + U_W

kernel_name = "retention_sgu_conformer_hybrid_step"


def rmsnorm(x, g):
    x32 = x.astype(jnp.float32)
    y = x32 * lax.rsqrt(jnp.mean(x32 * x32, axis=-1, keepdims=True) + RMS_EPS)
    return (y * g.astype(jnp.float32)).astype(x.dtype)


def layernorm32(x32, g, b):
    mu = jnp.mean(x32, axis=-1, keepdims=True)
    var = jnp.mean(jnp.square(x32 - mu), axis=-1, keepdims=True)
    return (x32 - mu) * lax.rsqrt(var + LN_EPS) * g.astype(jnp.float32) + b.astype(jnp.float32)


def rotary(x, pos):
    half = x.shape[-1] // 2
    inv = ROPE_BASE ** (-jnp.arange(half, dtype=jnp.float32) / half)
    ang = pos.astype(jnp.float32)[:, None] * inv[None, :]
    cos = jnp.cos(ang)[None, :, None, :]
    sin = jnp.sin(ang)[None, :, None, :]
    x32 = x.astype(jnp.float32)
    x1, x2 = x32[..., :half], x32[..., half:]
    return jnp.concatenate([x1 * cos - x2 * sin, x1 * sin + x2 * cos], axis=-1)


def retention(q, k, v, s0):
    b, t, h, _ = q.shape
    c = min(t, RET_CHUNK)
    nc = t // c

    def to_chunks(a):
        return a.reshape(b, nc, c, h, a.shape[-1]).transpose(1, 0, 3, 2, 4)

    log_g = jnp.log1p(-(2.0 ** (-5.0 - jnp.arange(RET_HEADS, dtype=jnp.float32))))
    idx = jnp.arange(c, dtype=jnp.float32)
    diff = idx[:, None] - idx[None, :]
    dmask = jnp.where(diff >= 0, jnp.exp(jnp.maximum(diff, 0.0)[None] * log_g[:, None, None]), 0.0)
    xi = jnp.exp((idx[None, :] + 1.0) * log_g[:, None])
    zeta = jnp.exp((c - 1.0 - idx[None, :]) * log_g[:, None])
    chunk_decay = jnp.exp(c * log_g)[:, None, None]

    def step(s, inp):
        qc, kc, vc = inp
        inner = jnp.einsum('bhnd,bhmd->bhnm', qc, kc) * dmask
        o = jnp.einsum('bhnm,bhme->bhne', inner, vc) \
            + jnp.einsum('bhnd,bhde->bhne', qc, s) * xi[None, :, :, None]
        s_new = chunk_decay * s + jnp.einsum('bhmd,bhme->bhde', kc * zeta[None, :, :, None], vc)
        return s_new, o

    s_fin, o = lax.scan(step, s0, (to_chunks(q), to_chunks(k), to_chunks(v)))
    o = o.transpose(1, 0, 3, 2, 4).reshape(b, t, h, v.shape[-1])
    return o, s_fin


def mixer_ret_sgu(h, pos, s0, w_in, w_out, gn_g, ln_g, ln_b, sgu_w, sgu_b):
    b, t, _ = h.shape
    z = h @ w_in
    q, k, v, g, u, s = jnp.split(z, SPLIT_IDX, axis=-1)
    q = rotary(q.reshape(b, t, RET_HEADS, RET_DK), pos)
    k = rotary(k.reshape(b, t, RET_HEADS, RET_DK), pos) * (RET_DK ** -0.5)
    v = v.reshape(b, t, RET_HEADS, RET_DV).astype(jnp.float32)
    o, s_new = retention(q, k, v, s0.astype(jnp.float32))
    mu = jnp.mean(o, axis=-1, keepdims=True)
    var = jnp.mean(jnp.square(o - mu), axis=-1, keepdims=True)
    o_n = ((o - mu) * lax.rsqrt(var + LN_EPS)).reshape(b, t, V_W) * gn_g.astype(jnp.float32)
    ret_out = (jax.nn.silu(g.astype(jnp.float32)) * o_n).astype(h.dtype)
    u = jax.nn.gelu(u, approximate=False)
    s = jax.nn.gelu(s, approximate=False).reshape(b, t, SGU_GROUPS, SGU_GD)
    s_n = layernorm32(s.astype(jnp.float32), ln_g.reshape(SGU_GROUPS, SGU_GD),
                      ln_b.reshape(SGU_GROUPS, SGU_GD)).astype(h.dtype)
    c = min(t, SGU_CHUNK)
    nc = t // c
    tri = jnp.tril(jnp.ones((c, c), dtype=bool))
    w_s = jnp.where(tri[None], sgu_w[:, :c, :c], 0.0).astype(h.dtype)
    bias = sgu_b[:, :c].T.astype(h.dtype)
    mixed = jnp.einsum('gij,bnjgd->bnigd', w_s, s_n.reshape(b, nc, c, SGU_GROUPS, SGU_GD)) \
        + bias[:, :, None]
    sgu_out = u * mixed.reshape(b, t, U_W)
    y = jnp.concatenate([ret_out, sgu_out], axis=-1) @ w_out
    return y, s_new, s_n.reshape(b, t, U_W)


def mixer_conv(h, hist, w_in, b_in, dw_w, dw_b, ln_g, ln_b, w_out, b_out):
    a, gt = jnp.split(h @ w_in + b_in, 2, axis=-1)
    xg = a * jax.nn.sigmoid(gt)
    full = jnp.concatenate([hist.astype(xg.dtype), xg], axis=1)
    conv = lax.conv_general_dilated(full, dw_w[:, None, :].astype(xg.dtype), (1,), 'VALID',
                                    dimension_numbers=('NWC', 'WIO', 'NWC'),
                                    feature_group_count=D_CONV) + dw_b
    c = jax.nn.silu(layernorm32(conv.astype(jnp.float32), ln_g, ln_b)).astype(h.dtype)
    y = c @ w_out + b_out
    new_hist = full[:, full.shape[1] - (CONV_WIDTH - 1):]
    return y, new_hist


def trunk(x, pos, ret_s0, conv_h0, norm_mix_g, norm_ffn_g, final_norm_g,
          ab_w_in, ab_w_out, ret_gn_g, sgu_ln_g, sgu_ln_b, sgu_w, sgu_b,
          conv_w_in, conv_b_in, conv_dw_w, conv_dw_b, conv_ln_g, conv_ln_b,
          conv_w_out, conv_b_out, ffn_w1, ffn_w2):
    ret_states, sgu_rows, conv_states = [], [], []
    for l in range(DEPTH):
        i = l // 2
        h = rmsnorm(x, norm_mix_g[l])
        if l % 2 == 0:
            y, s_new, rows = mixer_ret_sgu(h, pos, ret_s0[i], ab_w_in[i], ab_w_out[i], ret_gn_g[i],
                                           sgu_ln_g[i], sgu_ln_b[i], sgu_w[i], sgu_b[i])
            ret_states.append(s_new)
            sgu_rows.append(rows)
        else:
            y, hist = mixer_conv(h, conv_h0[i], conv_w_in[i], conv_b_in[i], conv_dw_w[i], conv_dw_b[i],
                                 conv_ln_g[i], conv_ln_b[i], conv_w_out[i], conv_b_out[i])
            conv_states.append(hist)
        x = x + y
        f = rmsnorm(x, norm_ffn_g[l]) @ ffn_w1[l]
        x = x + jnp.square(jax.nn.relu(f)) @ ffn_w2[l]
    return (rmsnorm(x, final_norm_g), jnp.stack(ret_states), jnp.stack(sgu_rows), jnp.stack(conv_states))


def setup_inputs(seed: int = 0) -> dict:
    key = jax.random.key(seed)
    ks = jax.random.split(key, 32)

    def nrm(k, shape, scale):
        return jax.random.normal(k, shape, jnp.float32) * scale

    return {
        "x_prompt": nrm(ks[0], (BATCH, SEQ, D_MODEL), 1.0),
        "x_sample": nrm(ks[1], (DEC_BATCH, DEC_SEQ, D_MODEL), 1.0),
        "state_ret": nrm(ks[2], (N_RET_LAYERS, DEC_BATCH, RET_HEADS, RET_DK, RET_DV), 0.5),
        "state_conv": nrm(ks[3], (N_CONV_LAYERS, DEC_BATCH, CONV_WIDTH - 1, D_CONV), 0.5),
        "norm_mix_g": 1.0 + nrm(ks[4], (DEPTH, D_MODEL), 0.02),
        "norm_ffn_g": 1.0 + nrm(ks[5], (DEPTH, D_MODEL), 0.02),
        "final_norm_g": 1.0 + nrm(ks[6], (D_MODEL,), 0.02),
        "ab_w_in": nrm(ks[7], (N_RET_LAYERS, D_MODEL, D_IN), D_MODEL ** -0.5),
        "ab_w_out": nrm(ks[8], (N_RET_LAYERS, D_MIX_OUT, D_MODEL), D_MIX_OUT ** -0.5),
        "ret_gn_g": 1.0 + nrm(ks[9], (N_RET_LAYERS, V_W), 0.02),
        "sgu_ln_g": 1.0 + nrm(ks[10], (N_RET_LAYERS, U_W), 0.02),
        "sgu_ln_b": nrm(ks[11], (N_RET_LAYERS, U_W), 0.02),
        "sgu_w": nrm(ks[12], (N_RET_LAYERS, SGU_GROUPS, SGU_CHUNK, SGU_CHUNK), SGU_CHUNK ** -0.5),
        "sgu_b": 1.0 + nrm(ks[13], (N_RET_LAYERS, SGU_GROUPS, SGU_CHUNK), 0.1),
        "conv_w_in": nrm(ks[14], (N_CONV_LAYERS, D_MODEL, 2 * D_CONV), D_MODEL ** -0.5),
        "conv_b_in": nrm(ks[15], (N_CONV_LAYERS, 2 * D_CONV), 0.02),
        "conv_dw_w": nrm(ks[16], (N_CONV_LAYERS, CONV_WIDTH, D_CONV), CONV_WIDTH ** -0.5),
        "conv_dw_b": nrm(ks[17], (N_CONV_LAYERS, D_CONV), 0.02),
        "conv_ln_g": 1.0 + nrm(ks[18], (N_CONV_LAYERS, D_CONV), 0.02),
        "conv_ln_b": nrm(ks[19], (N_CONV_LAYERS, D_CONV), 0.02),
        "conv_w_out": nrm(ks[20], (N_CONV_LAYERS, D_CONV, D_MODEL), D_CONV ** -0.5),
        "conv_b_out": nrm(ks[21], (N_CONV_LAYERS, D_MODEL), 0.02),
        "ffn_w1": nrm(ks[22], (DEPTH, D_MODEL, D_FF), D_MODEL ** -0.5),
        "ffn_w2": nrm(ks[23], (DEPTH, D_FF, D_MODEL), D_FF ** -0.5),
    }


def reference(x_prompt, x_sample, state_ret, state_conv, norm_mix_g, norm_ffn_g, final_norm_g,
              ab_w_in, ab_w_out, ret_gn_g, sgu_ln_g, sgu_ln_b, sgu_w, sgu_b,
              conv_w_in, conv_b_in, conv_dw_w, conv_dw_b, conv_ln_g, conv_ln_b,
              conv_w_out, conv_b_out, ffn_w1, ffn_w2):
    weights = (norm_mix_g, norm_ffn_g, final_norm_g, ab_w_in, ab_w_out, ret_gn_g, sgu_ln_g, sgu_ln_b,
               sgu_w, sgu_b, conv_w_in, conv_b_in, conv_dw_w, conv_dw_b, conv_ln_g, conv_ln_b,
               conv_w_out, conv_b_out, ffn_w1, ffn_w2)
    pos_p = jnp.arange(SEQ)
    ret0_p = jnp.zeros((N_RET_LAYERS, x_prompt.shape[0], RET_HEADS, RET_DK, RET_DV), jnp.float32)
    conv0_p = jnp.zeros((N_CONV_LAYERS, x_prompt.shape[0], CONV_WIDTH - 1, D_CONV), x_prompt.dtype)
    y_prompt, ret_state_prompt, _, conv_state_prompt = trunk(x_prompt, pos_p, ret0_p, conv0_p, *weights)
    pos_s = PAST_LEN + jnp.arange(x_sample.shape[1])
    y_sample, ret_state_sample, sgu_rows_sample, conv_state_sample = trunk(
        x_sample, pos_s, state_ret, state_conv, *weights)
    return (y_prompt, y_sample, ret_state_prompt, ret_state_sample, sgu_rows_sample,
            conv_state_prompt, conv_state_sample)
```

```python
import numpy as np
from contextlib import ExitStack
import concourse.bass as bass
import concourse.mybir as mybir
from concourse.bass_utils import run_bass_kernel_spmd

F32 = mybir.dt.float32
BF16 = mybir.dt.bfloat16
AF = mybir.ActivationFunctionType
ALU = mybir.AluOpType

D = 1024
KD = 8
SEQ = 2048
P = 1024
NPASS = SEQ // P
NS = 16
NT = P + NS
PAST = 16384
RMS_EPS = 1e-6
LN_EPS = 1e-5
NVEC = 42
CW = 31


class Sched:
    def __init__(self):
        self.ops = []
        self.lastw = {}
        self.readers = {}
        self.chan_n = {}
        self.barrier_ops = []
        self.first_after = {}
        self.bar_pos = 0

    def add(self, eng, fn, r=(), w=(), chan=None):
        if STRICT:
            def nk(k):
                if isinstance(k, str) and k.startswith("PT"):
                    return "PT"
                if COARSE_POPK and isinstance(k, tuple) and k[0] in ("PO", "PK"):
                    return k[0]
                return k
            r = [nk(k) for k in r]
            w = [nk(k) for k in w]
        oid = len(self.ops)
        deps = {}

        def dep(o, kind):
            if o is not None:
                deps[o] = deps.get(o, 0) | kind

        for k in r:
            dep(self.lastw.get(k), 1)
        for k in w:
            lw = self.lastw.get(k)
            if not (lw is not None and chan is not None and self.ops[lw]['chan'] == chan):
                dep(lw, 2)
            rd = self.readers.get(k)
            if rd:
                for o in rd.values():
                    dep(o, 4)
        if eng not in self.first_after:
            for o in self.barrier_ops:
                dep(o, 1)
            self.first_after[eng] = oid
        for k in w:
            self.lastw[k] = oid
            self.readers[k] = {}
        ws = set(w)
        for k in r:
            if k not in ws:
                rd = self.readers.setdefault(k, {})
                if chan is not None:
                    rd[('dma', oid)] = oid
                else:
                    rd[eng] = oid
        cnt = None
        if chan is not None:
            self.chan_n[chan] = self.chan_n.get(chan, 0) + 1
            cnt = self.chan_n[chan]
        self.ops.append(dict(eng=eng, fn=fn, deps=deps, chan=chan, cnt=cnt, has_r=bool(r)))
        return oid

    def barrier(self):
        b = []
        seen = set()
        for i in range(len(self.ops) - 1, self.bar_pos - 1, -1):
            o = self.ops[i]
            if o['chan'] is not None:
                if o['has_r']:
                    b.append(i)
            elif o['eng'] not in seen:
                seen.add(o['eng'])
                b.append(i)
        for i in self.barrier_ops:
            o = self.ops[i]
            if o['chan'] is None and o['eng'] not in seen:
                seen.add(o['eng'])
                b.append(i)
        self.barrier_ops = b
        self.first_after = {}
        self.bar_pos = len(self.ops)

    def emit(self, nc, block, sems, chan_sems, wait_all_chans, final_chans):
        ops = self.ops
        n = len(ops)
        sig = [False] * n
        for o in ops:
            for d, kind in o['deps'].items():
                p = ops[d]
                if p['chan'] is not None:
                    continue
                if p['eng'] == o['eng'] and o['chan'] is None:
                    if o['eng'] == 'pe' or (not (kind & 1) and not STRICT_SE):
                        continue
                sig[d] = True
        sigidx = [0] * n
        cnt = {}
        for i, o in enumerate(ops):
            if o['chan'] is None and sig[i]:
                cnt[o['eng']] = cnt.get(o['eng'], 0) + 1
                sigidx[i] = cnt[o['eng']]
        self.n_sig = dict(cnt)
        by_eng = {}
        for i, o in enumerate(ops):
            by_eng.setdefault(o['eng'], []).append(i)

        def run(engname, e):
            waited = {}
            for i in by_eng.get(engname, ()):
                o = ops[i]
                need = {}
                for d, kind in o['deps'].items():
                    p = ops[d]
                    if p['chan'] is not None:
                        c = p['chan']
                        v = 16 * (self.chan_n[c] if c in wait_all_chans else p['cnt'])
                        key = ('c', c)
                    else:
                        if p['eng'] == o['eng'] and o['chan'] is None:
                            if o['eng'] == 'pe' or (not (kind & 1) and not STRICT_SE):
                                continue
                        v = sigidx[d]
                        key = ('e', p['eng'])
                    if v > need.get(key, 0):
                        need[key] = v
                for key, v in need.items():
                    if waited.get(key, 0) >= v:
                        continue
                    waited[key] = v
                    s = chan_sems[key[1]] if key[0] == 'c' else sems[key[1]]
                    e.wait_ge(s, v)
                ins = o['fn'](e)
                if o['chan'] is not None:
                    ins.then_inc(chan_sems[o['chan']], 16)
                elif sig[i]:
                    ins.then_inc(sems[o['eng']], 1)
            if engname == 'sp':
                for c in final_chans:
                    if self.chan_n.get(c, 0):
                        e.wait_ge(chan_sems[c], 16 * self.chan_n[c])

        @block.tensor
        def _(e):
            run('pe', e)

        @block.scalar
        def _(e):
            run('act', e)

        @block.vector
        def _(e):
            run('dve', e)

        @block.gpsimd
        def _(e):
            run('pool', e)

        @block.sync
        def _(e):
            run('sp', e)


def _consts():
    log_g = np.log1p(-(2.0 ** (-5.0 - np.arange(4, dtype=np.float32)))).astype(np.float32)
    idx = np.arange(128, dtype=np.float32)
    sc = np.float32(128 ** -0.5)
    diff = idx[None, :] - idx[:, None]
    dm = np.zeros((128, 4, 128), np.float32)
    for h in range(4):
        dm[:, h, :] = np.where(diff >= 0, np.exp(np.maximum(diff, 0.0) * log_g[h]), 0.0) * sc
    zeta = np.zeros((128, 4), np.float32)
    xi = np.zeros((128, 4, 128), np.float32)
    for h in range(4):
        zeta[:, h] = np.exp((127.0 - idx) * log_g[h]) * sc
        x1 = np.exp((idx + 1.0) * log_g[h])
        xi[:, h, :] = x1[None, :]
    decay = [float(np.exp(np.float32(128.0) * log_g[h])) for h in range(4)]
    gam = [float(np.exp(log_g[h])) for h in range(4)]
    half = 64
    inv = (10000.0 ** (-np.arange(half, dtype=np.float32) / half)).astype(np.float32)
    pos = np.concatenate([np.arange(SEQ), np.full(NS, PAST)]).astype(np.float32)
    ang = (pos[:, None] * inv[None, :]).astype(np.float32)
    c = np.cos(ang).astype(np.float32).T
    s = np.sin(ang).astype(np.float32).T
    ropec = np.concatenate([c, c], axis=0)
    ropes = np.concatenate([-s, s], axis=0)
    ident = np.eye(128, dtype=np.float32)
    tri = (idx[None, :] >= idx[:, None]).astype(np.float32)
    ind = np.zeros((120, 4, 16), np.float32)
    for t in range(4):
        for bb in range(4):
            ind[bb * 30:(bb + 1) * 30, t, 4 * t + bb] = 1.0
    return dict(dmask=dm, zeta=zeta, xitab=xi, ropec=np.ascontiguousarray(ropec),
                ropes=np.ascontiguousarray(ropes), ident=ident, tri=tri,
                ind=ind.reshape(120, 64)), decay, gam


DEBUG = False
DM_BCAST = True
import os
STRICT = os.environ.get('KSTRICT', '1') == '1'
COARSE_POPK = os.environ.get('KCOARSE_POPK', '1') == '1'
STRICT_SE = os.environ.get('KSTRICT_SE', '1') == '1'


VR0_DWW = 11


def build_nc():
    consts, decay, gam = _consts()
    nc = bass.Bass("TRN2", target_bir_lowering=False)

    def din(name, shape):
        return nc.dram_tensor(name, list(shape), F32, kind="ExternalInput").ap()

    def dout(name, shape):
        return nc.dram_tensor(name, list(shape), F32, kind="ExternalOutput").ap()

    xp = din("xp", [SEQ, D]); xs = din("xs", [NS, D])
    sret = din("sret", [NS, 4, 128, 256]); sconv = din("sconv", [NS * 30, D])
    win = din("win", [D, 5120]); wout = din("wout", [2048, D])
    gng = din("gng", [1, D]); slg = din("slg", [1, D]); slb = din("slb", [1, D])
    sguw = din("sguw", [4, 128, 128]); sgub_d = din("sgub", [1, 512])
    cwin = din("cwin", [D, 2048]); cwout = din("cwout", [D, D])
    w1 = din("w1", [2, D, 4096]); w2 = din("w2", [2, 4096, D])
    vecs_d = din("vecs", [NVEC, D])
    c_dm = din("dmask", [128, 512]); c_zeta = din("zeta", [128, 4]); c_xi = din("xitab", [128, 512])
    c_rc = din("ropec", [128, SEQ + NS]); c_rs = din("ropes", [128, SEQ + NS])
    c_id = din("ident", [128, 128]); c_tri = din("tri", [128, 128]); c_ind = din("ind", [120, 64])

    yp = dout("yp", [SEQ, D]); ys = dout("ys", [NS, D])
    rsp = dout("rsp", [4, 128, 256]); rss = dout("rss", [NS, 4, 128, 256])
    sgur = dout("sgur", [NS, D]); csp = dout("csp", [30, D]); css = dout("css", [NS * 30, D])

    dbg = dout("dbg", [8 * KD, 128, NT]) if DEBUG else None
    S = Sched()
    es = ExitStack()
    with es:
        def sb(name, shape, dt=F32):
            return es.enter_context(nc.sbuf_tensor("sb_" + name, list(shape), dt))

        def ps(name, shape, dt=F32):
            return es.enter_context(nc.psum_tensor("ps_" + name, list(shape), dt))

        xT = sb("xT", [128, KD, NT]); hT = sb("hT", [128, KD, NT], BF16)
        identf = sb("identf", [128, 128]); identb = sb("identb", [128, 128], BF16)
        onesb = sb("onesb", [128, 128], BF16); onesf = sb("onesf", [1, 128])
        tri = sb("tri", [128, 128]); ind = sb("ind", [120, 64])
        vecT = sb("vecT", [128, KD, NVEC]); zeta = sb("zeta", [128, 4])
        WsT = sb("WsT", [128, 4, 128], BF16); sgub = sb("sgub", [1, 512])
        bias_s = sb("bias_s", [1, 64]); w00 = sb("w00", [16, 4]); Wdiag = sb("Wdiag", [16, 64], BF16)
        S_f = sb("S_f", [128, 4, 256]); S_b = sb("S_b", [128, 4, 256], BF16)
        hist = sb("hist", [128, KD, 30], BF16)
        zer16 = sb("zer16", [1, 16])
        neghalf = sb("neghalf", [128, 4])
        biasb = sb("biasb", [128, 4, 128])
        hconv = sb("hconv", [128, KD * NS])
        WB = [sb(f"WB{i}", [128, 8192], BF16) for i in range(4)]
        WS = [sb(f"WS{i}", [128, 2048], BF16) for i in range(2)]
        PB = [ps(f"PB{i}", [128, 512]) for i in range(3)]
        PO = ps("PO", [128, 1024])
        PK = ps("PK", [128, 1024])
        PT = ps("PT", [128, 1024], BF16)

        sem_names = ['pe', 'act', 'dve', 'pool', 'sp']
        sems = {n_: es.enter_context(nc.semaphore("s_" + n_)) for n_ in sem_names}

        st = dict(pb=0, wb=0, ws=0, pt=0, xin=0, so=0)

        def nextpb():
            i = st['pb'] % 3
            st['pb'] += 1
            return PB[i], f"PB{i}"

        def nextpt():
            i = st['pt'] % 2
            st['pt'] += 1
            return PT[:, i * 512:(i + 1) * 512], f"PT{i}"

        def nextwb():
            i = st['wb'] % 4
            st['wb'] += 1
            return WB[i], f"WB{i}", f"wb{i}"

        def nextws():
            i = st['ws'] % 2
            st['ws'] += 1
            return WS[i], f"WS{i}", f"ws{i}"

        def mm(out, lhsT, rhs, start, stop, r, w):
            S.add('pe', lambda e: e.matmul(out, lhsT, rhs, start=start, stop=stop), r, w)

        def tp(out, in_, ident, r, w):
            S.add('pe', lambda e: e.transpose(out, in_, ident), r, w)

        def act(out, in_, func, r, w, bias=None, scale=None):
            kw = {}
            if bias is not None:
                kw['bias'] = bias
            if scale is not None:
                kw['scale'] = scale
            S.add('act', lambda e: e.activation(out, in_, func, **kw), r, w)

        def tt(eng, out, in0, in1, op, r, w):
            S.add(eng, lambda e: e.tensor_tensor(out, in0, in1, op), r, w)

        def ts(eng, out, in0, s1, s2, op0, op1, r, w):
            if s2 is None:
                S.add(eng, lambda e: e.tensor_scalar(out, in0, s1, None, op0), r, w)
            else:
                S.add(eng, lambda e: e.tensor_scalar(out, in0, s1, s2, op0, op1), r, w)

        def stt(out, in0, sc, in1, op0, op1, r, w):
            S.add('dve', lambda e: e.scalar_tensor_tensor(out, in0, sc, in1, op0, op1), r, w)

        def cp(eng, out, in_, r, w):
            if eng == 'act':
                S.add('act', lambda e: e.activation(out, in_, AF.Copy), r, w)
            else:
                S.add(eng, lambda e: e.tensor_copy(out, in_), r, w)

        def recip(out, in_, r, w):
            S.add('dve', lambda e: e.reciprocal(out, in_), r, w)

        def bnstats(out, in_, r, w):
            S.add('dve', lambda e: e.bn_stats(out, in_), r, w)

        def bnaggr(out, in_, r, w):
            S.add('dve', lambda e: e.bn_aggr(out, in_), r, w)

        def memset(ap, val, w):
            S.add('dve', lambda e: e.memset(ap, val), [], w)

        def rstd_pool(out, tmp, var_ap, eps, r, tk, w):
            np_ = out.shape[0]
            nco = out.shape[1]
            S.add('pool', lambda e: e.tensor_scalar(tmp, var_ap, float(eps), 1.0, ALU.add, ALU.mult), r, [tk])
            S.add('pool', lambda e: e.tensor_tensor(out, tmp, neghalf[0:np_, 0:nco], ALU.pow), [tk, "neghalf"], w)

        def dma(q, out, in_, r, w, chan):
            S.add(q, lambda e: e.dma_start(out=out, in_=in_), r, w, chan=chan)

        def xk(k, tgi):
            return ("xT", k, tgi)

        def dumpt(tag, ap, keys, shape, dt=F32):
            if DEBUG:
                t = nc.dram_tensor("dbg_" + tag, list(shape), dt, kind="ExternalOutput").ap()
                dma('sp', t, ap, keys, [], 'o_dbg')

        def dump(idx):
            if DEBUG:
                for k in range(KD):
                    dma('sp', dbg[idx * KD + k], xT[:, k, :], [xk(k, t) for t in range(3)], [], 'o_dbg')

        def hk(tgi):
            return ("hT", tgi)

        dma('sp', identf[:, :], c_id, [], ["identf"], 'const')
        dma('sp', tri[:, :], c_tri, [], ["tri"], 'const')
        dma('sp', ind[:, :], c_ind, [], ["ind"], 'const')
        dma('sp', zeta[:, :], c_zeta, [], ["zeta"], 'const')
        dma('sp', sgub[:, :], sgub_d, [], ["sgub"], 'const')
        dma('sp', biasb[:, :, :].rearrange("p g i -> p (g i)"), sgub_d[0:1, :].partition_broadcast(128), [], ["biasb"], 'const')
        for g in range(4):
            dma('sp', w00[:, g:g + 1], sguw[g, 0:1, 0:1].partition_broadcast(16), [], ["w00"], 'const')
        S.add('dve', lambda e: e.tensor_copy(identb[:, :], identf[:, :]), ["identf"], ["identb"])
        S.add('dve', lambda e: e.memset(onesb[:, :], 1.0), [], ["onesb"])
        S.add('dve', lambda e: e.memset(onesf[:, :], 1.0), [], ["onesf"])
        S.add('dve', lambda e: e.memset(zer16[:, :], 0.0), [], ["zer16"])
        S.add('dve', lambda e: e.memset(neghalf[:, :], -0.5), [], ["neghalf"])
        S.add('dve', lambda e: e.memset(S_f[:, :, :], 0.0), [], [("S_f", h) for h in range(4)])
        S.add('dve', lambda e: e.memset(S_b[:, :, :], 0.0), [], [("S_b", h) for h in range(4)])
        S.add('dve', lambda e: e.memset(hist[:, :, :], 0.0), [], ["hist"])
        for g in range(4):
            ts('dve', bias_s[0:1, g * 16:(g + 1) * 16], zer16[0:1, :], sgub[0:1, g * 128:g * 128 + 1], None,
               ALU.add, None, ["zer16", "sgub"], ["bias_s"])
            ts('dve', Wdiag[:, g * 16:(g + 1) * 16], identf[0:16, 0:16], w00[:, g:g + 1], None,
               ALU.mult, None, ["identf", "w00"], ["Wdiag"])
        with ExitStack() as ph:
            vecs = ph.enter_context(nc.sbuf_tensor("sb_vecs", [NVEC, D], F32))
            swt = ph.enter_context(nc.sbuf_tensor("sb_swt", [128, 512], F32))
            dma('sp', vecs[:, :], vecs_d, [], ["vecs"], 'const')
            dma('sp', swt[:, :].rearrange("p (g j) -> p g j", g=4), sguw.rearrange("g i j -> i g j"),
                [], ["swt"], 'const')
            pb, pk = nextpb()
            for k in range(KD):
                tp(pb[:, k * NVEC:(k + 1) * NVEC], vecs[:, k * 128:(k + 1) * 128], identf[0:NVEC, 0:NVEC],
                   ["vecs", "identf"], [pk])
            cp('act', vecT[:, :, :], pb[:, 0:KD * NVEC].rearrange("p (k r) -> p k r", k=KD), [pk], ["vecT"])
            pb, pk = nextpb()
            for g in range(4):
                tp(pb[:, g * 128:(g + 1) * 128], swt[:, g * 128:(g + 1) * 128], identf[:, :], ["swt", "identf"], [pk])
            for g in range(4):
                tt('dve', WsT[:, g, :], pb[:, g * 128:(g + 1) * 128], tri[:, :], ALU.mult, [pk, "tri"], ["WsT"])
            dma('sp', css.rearrange("(b j) d -> b j d", j=30)[:, 0:29, :],
                sconv.rearrange("(b j) d -> b j d", j=30)[:, 1:30, :], [], [], 'o_css')
            hs_ = [ph.enter_context(nc.sbuf_tensor(f"sb_hs{i}", [120, D], F32)) for i in range(2)]
            wrep = ph.enter_context(nc.sbuf_tensor("sb_wrep", [120, D], F32))
            for r4 in range(4):
                dma('sp', wrep[r4 * 30:(r4 + 1) * 30, :], vecs_d[VR0_DWW:VR0_DWW + 30, :], [], ["wrep"], 'const')
            for t4 in range(4):
                hi = t4 % 2
                dma('sp', hs_[hi][:, :], sconv[t4 * 120:(t4 + 1) * 120, :], [], [f"hs{hi}"], f't_hs{hi}')
                tt('dve', hs_[hi][:, :], hs_[hi][:, :], wrep[:, :], ALU.mult, [f"hs{hi}", "wrep"], [f"hs{hi}"])
                for cc in range(KD):
                    mm(PO[:, cc * NS:(cc + 1) * NS], hs_[hi][:, cc * 128:(cc + 1) * 128],
                       ind[:, t4 * 16:(t4 + 1) * 16], t4 == 0 and cc == 0, t4 == 3 and cc == KD - 1,
                       [f"hs{hi}", "ind"], [("PO", 0)])
            cp('act', hconv[:, :], PO[:, 0:KD * NS], [("PO", 0)], ["hconv"])
            S.barrier()

        pre = {}
        VR = dict(nm0=0, nm1=1, nf0=2, nf1=3, fin=4, cba=5, cbg=6, dwb=7, clg=8, clb=9, cbo=10, dww=11)

        for p in range(NPASS):
            last = (p == NPASS - 1)
            tgs = [(0, 512), (512, 512)] + ([(P, NS)] if last else [])

            def load_x_pre(pp, ph_):
                xin = [ph_.enter_context(nc.sbuf_tensor(f"xin{i}_{pp}", [128, D], F32)) for i in range(2)]
                for c in range(2):
                    dma('sp', xin[c][:, :], xp[pp * P + c * 128:pp * P + (c + 1) * 128, :], [], [f"xin{c}"], f'xin{c}')
                return xin

            def load_x(pp, ph_, xin=None):
                lastp = (pp == NPASS - 1)
                pre = xin is not None
                if not pre:
                    xin = [ph_.enter_context(nc.sbuf_tensor(f"xin{i}_{pp}", [128, D], F32)) for i in range(2)]
                for c in range(P // 128):
                    i = c % 2
                    if not (pre and c < 2):
                        dma('sp', xin[i][:, :], xp[pp * P + c * 128:pp * P + (c + 1) * 128, :], [], [f"xin{i}"], f'xin{i}')
                    for half in range(2):
                        pb, pk = nextpb()
                        for kk in range(4):
                            k = half * 4 + kk
                            tp(pb[:, kk * 128:(kk + 1) * 128], xin[i][:, k * 128:(k + 1) * 128], identf[:, :],
                               [f"xin{i}", "identf"], [pk])
                        cp('act' if half == 0 else 'dve', xT[:, half * 4:half * 4 + 4, c * 128:(c + 1) * 128],
                           pb[:, :].rearrange("p (k t) -> p k t", k=4), [pk],
                           [xk(k, c // 4) for k in range(half * 4, half * 4 + 4)])
                if lastp:
                    dma('sp', xin[0][0:NS, :], xs, [], ["xin0"], 'xin0')
                    pb, pk = nextpb()
                    for k in range(KD):
                        tp(pb[:, k * NS:(k + 1) * NS], xin[0][0:NS, k * 128:(k + 1) * 128], identf[0:NS, 0:NS],
                           ["xin0", "identf"], [pk])
                    cp('act', xT[:, :, P:P + NS], pb[:, 0:KD * NS].rearrange("p (k t) -> p k t", k=KD), [pk],
                       [xk(k, 2) for k in range(KD)])

            if p == 0:
                with ExitStack() as ph:
                    load_x(0, ph)
                    S.barrier()

            def rmsnorm(vrow, ph, tag, ext=None):
                if ext is None:
                    sq = ph.enter_context(nc.sbuf_tensor(f"sq{tag}", [128, KD, 512], BF16))
                    sd = [ph.enter_context(nc.sbuf_tensor(f"sd{tag}{i}", [128, 512], F32)) for i in range(2)]
                    rs = [ph.enter_context(nc.sbuf_tensor(f"rs{tag}{i}", [128, 512], F32)) for i in range(2)]
                    sqk, sdk, rsk = ["sq"], ["sd0", "sd1"], ["rs0", "rs1"]
                else:
                    sq, sd1, rs1, sqk, sdk1, rsk1 = ext
                    sd = [sd1, sd1]; rs = [rs1, rs1]; sdk = [sdk1, sdk1]; rsk = [rsk1, rsk1]
                for tgi, (c0, n) in enumerate(tgs):
                    i = tgi % 2
                    act(sq[:, :, 0:n], xT[:, :, c0:c0 + n], AF.Square, [xk(k, tgi) for k in range(KD)], sqk)
                    pb, pk = nextpb()
                    for k in range(KD):
                        mm(pb[:, 0:n], onesb[:, :], sq[:, k, 0:n], k == 0, k == KD - 1, [sqk[0], "onesb"], [pk])
                    act(sd[i][:, 0:n], pb[:, 0:n], AF.Sqrt, [pk], [sdk[i]], bias=float(RMS_EPS), scale=1.0 / D)
                    recip(rs[i][:, 0:n], sd[i][:, 0:n], [sdk[i]], [rsk[i]])
                    for k in range(KD):
                        stt(hT[:, k, c0:c0 + n], xT[:, k, c0:c0 + n], vecT[:, k, vrow:vrow + 1], rs[i][:, 0:n],
                            ALU.mult, ALU.mult, [xk(k, tgi), "vecT", rsk[i]], [hk(tgi)])

            def wload(dst, src, wkey, chan):
                dma('pool', dst, src, [], [wkey], chan)

            def out_proj(wo, wokey, rT, rkeys, tgi, c0, n):
                for d in range(KD):
                    pb, pk = nextpb()
                    for kk in range(2):
                        mm(pb[:, 0:n], wo[:, kk * 1024 + d * 128:kk * 1024 + (d + 1) * 128], rT[:, kk, 0:n],
                           kk == 0, kk == 1, [wokey] + rkeys, [pk])
                    tt('dve', xT[:, d, c0:c0 + n], xT[:, d, c0:c0 + n], pb[:, 0:n], ALU.add,
                       [xk(d, tgi), pk], [xk(d, tgi)])

            def ffn_load(l, j):
                wa, wak, wac = nextwb()
                wb_, wbk, wbc = nextwb()
                wa3 = wa[:, :].rearrange("p (k c) -> p k c", k=KD)
                wb3 = wb_[:, :].rearrange("p (k c) -> p k c", k=KD)
                for hh in range(2):
                    wload(wa3[:, hh * 4:hh * 4 + 4, :],
                          w1[l].rearrange("(k p) c -> p k c", p=128)[:, hh * 4:hh * 4 + 4, j * 1024:(j + 1) * 1024],
                          wak, wac)
                for hh in range(2):
                    wload(wb3[:, hh * 4:hh * 4 + 4, :],
                          w2[l][j * 1024:(j + 1) * 1024, :].rearrange("(k p) c -> p k c", p=128)[:, hh * 4:hh * 4 + 4, :],
                          wbk, wbc)
                return wa3, wak, wb3, wbk

            def conv_load_ab():
                wa, wak, wac = nextwb()
                wb_, wbk, wbc = nextwb()
                wa3 = wa[:, :].rearrange("p (k c) -> p k c", k=KD)
                wb3 = wb_[:, :].rearrange("p (k c) -> p k c", k=KD)
                cw3_ = cwin.rearrange("(k p) c -> p k c", p=128)
                for hh in range(2):
                    wload(wa3[:, hh * 4:hh * 4 + 4, :], cw3_[:, hh * 4:hh * 4 + 4, 0:1024], wak, wac)
                for hh in range(2):
                    wload(wb3[:, hh * 4:hh * 4 + 4, :], cw3_[:, hh * 4:hh * 4 + 4, 1024:2048], wbk, wbc)
                pre['conv'] = (wa3, wak, wb3, wbk)

            def ffn(l, ph, tag, pre0=None, hook3=None):
                fT = [ph.enter_context(nc.sbuf_tensor(f"fT{tag}{i}", [128, KD, 512 if i < 2 else NS], BF16))
                      for i in range(len(tgs))]
                rl = [ph.enter_context(nc.sbuf_tensor(f"rl{tag}{i}", [128, 512], F32)) for i in range(2)]
                for j in range(4):
                    wa3, wak, wb3, wbk = pre0 if (j == 0 and pre0 is not None) else ffn_load(l, j)
                    if j == 3 and hook3 is not None:
                        hook3()
                    for tgi, (c0, n) in enumerate(tgs):
                        for fk in range(KD):
                            pb, pk = nextpb()
                            for k in range(KD):
                                mm(pb[:, 0:n], wa3[:, k, fk * 128:(fk + 1) * 128], hT[:, k, c0:c0 + n], k == 0, k == KD - 1,
                                   [wak, hk(tgi)], [pk])
                            ri = fk % 2
                            act(rl[ri][:, 0:n], pb[:, 0:n], AF.Relu, [pk], [f"rl{ri}"])
                            tt('dve', fT[tgi][:, fk, 0:n], rl[ri][:, 0:n], rl[ri][:, 0:n], ALU.mult, [f"rl{ri}"], [(f"fT{tgi}", fk)])
                    for tgi, (c0, n) in enumerate(tgs):
                        for d in range(KD):
                            pb, pk = nextpb()
                            for fk in range(KD):
                                mm(pb[:, 0:n], wb3[:, fk, d * 128:(d + 1) * 128], fT[tgi][:, fk, 0:n], fk == 0, fk == KD - 1,
                                   [wbk] + [(f"fT{tgi}", fk)], [pk])
                            tt('dve', xT[:, d, c0:c0 + n], xT[:, d, c0:c0 + n], pb[:, 0:n], ALU.add,
                               [xk(d, tgi), pk], [xk(d, tgi)])

            with ExitStack() as ph:
                rmsnorm(VR['nm0'], ph, f"a{p}")
                S.barrier()

            with ExitStack() as ph:
                def T(name, shape, dt=F32):
                    return ph.enter_context(nc.sbuf_tensor(f"sb_{name}_{p}", list(shape), dt))
                cosT = T("cosT", [128, NT]); sinT = T("sinT", [128, NT])
                xitab = T("xitab", [128, 4, 128]); dmask = T("dmask", [128, 4, 128])
                gngb = [T(f"gngb{i}", [128, 256]) for i in range(2)]
                t1 = T("t1", [128, 512]); t2 = T("t2", [128, 512])
                qT = [T(f"qT{i}", [128, 512], BF16) for i in range(2)]
                kT = [T(f"kT{i}", [128, 512], BF16) for i in range(2)]
                qx = [T(f"qx{i}", [128, 512], BF16) for i in range(2)]
                v4 = [T(f"v4{i}", [128, 4, 256], BF16) for i in range(2)]
                gs4 = [T(f"gs4{i}", [128, 4, 256], BF16) for i in range(3)]
                inm4 = [T("inm40", [128, 4, 128], BF16)] * 2
                kz4 = [T("kz40", [128, 4, 128], BF16)] * 2
                Sb4 = [T("Sb40", [128, 4, 256], BF16)] * 2
                osb4 = [T(f"osb4{i}", [128, 4, 256]) for i in range(2)]
                rr4 = [T(f"rr4{i}", [128, 4, 256], BF16) for i in range(2)]
                rT = [T(f"rT{i}", [128, 2, 512], BF16) for i in range(2)]
                bst4 = [T(f"bst4{i}", [128, 4, 6]) for i in range(2)]
                mv4 = [T(f"mv4{i}", [128, 4, 2]) for i in range(2)]
                sdv4 = [T(f"sdv4{i}", [128, 4]) for i in range(2)]
                rsv4 = [T(f"rsv4{i}", [128, 4]) for i in range(2)]
                if last:
                    qsf = T("qsf", [128, NS]); ksf = T("ksf", [128, NS]); qm = T("qm", [128, 272])
                    kTM = T("kTM", [NS, 128]); km4 = T("km4", [NS, 4, 128])
                    vsf = T("vsf", [NS, 256])
                    so = [T(f"so{i}", [128, 256]) for i in range(4)]; sn = [T(f"sn{i}", [128, 256]) for i in range(4)]
                    gs_s = T("gs_s", [NS, 256]); osb_s = T("osb_s", [NS, 256]); rr_s = T("rr_s", [NS, 256], BF16)
                    rT_s = T("rT_s", [128, 2, NS], BF16)
                    bst_s = T("bst_s", [NS, 6]); mv_s = T("mv_s", [NS, 2]); sdv_s = T("sdv_s", [NS, 1]); rsv_s = T("rsv_s", [NS, 1])

                dma('sp', cosT[:, 0:P], c_rc[:, p * P:(p + 1) * P], [], ["cosT"], 't_cos')
                dma('sp', sinT[:, 0:P], c_rs[:, p * P:(p + 1) * P], [], ["sinT"], 't_sin')
                if last:
                    dma('sp', cosT[:, P:NT], c_rc[:, SEQ:SEQ + NS], [], ["cosT"], 't_cos')
                    dma('sp', sinT[:, P:NT], c_rs[:, SEQ:SEQ + NS], [], ["sinT"], 't_sin')
                dma('sp', xitab[:, :, :].rearrange("p h n -> p (h n)"), c_xi, [], ["xitab"], 't_xi')
                dma('sp', dmask[:, :, :].rearrange("p h n -> p (h n)"), c_dm, [], ["dmask"], 't_dm')
                if last:
                    memset(qm[:, :], 0.0, ["qm"])

                win3 = win.rearrange("(k p) c -> p k c", p=128)
                heads = {}
                hwo = {}

                def load_wh(h):
                    wh, whk, whc = nextwb()
                    wh3 = wh[:, :].rearrange("p (k c) -> p k c", k=KD)
                    wload(wh3[:, :, 0:128], win3[:, :, h * 128:(h + 1) * 128], whk, whc)
                    wload(wh3[:, :, 128:256], win3[:, :, 512 + h * 128:512 + (h + 1) * 128], whk, whc)
                    wload(wh3[:, :, 256:512], win3[:, :, 1024 + h * 256:1024 + (h + 1) * 256], whk, whc)
                    wload(wh3[:, :, 512:768], win3[:, :, 2048 + h * 256:2048 + (h + 1) * 256], whk, whc)
                    cp('act', wh3[:, :, 768:832], wh3[:, :, 64:128], [whk], [whk + "sq"])
                    cp('act', wh3[:, :, 832:896], wh3[:, :, 0:64], [whk], [whk + "sq"])
                    cp('dve', wh3[:, :, 896:960], wh3[:, :, 192:256], [whk], [whk + "sk"])
                    cp('dve', wh3[:, :, 960:1024], wh3[:, :, 128:192], [whk], [whk + "sk"])
                    heads[h] = (wh3, [whk, whk + "sq", whk + "sk"])

                def load_wo(h):
                    wo, wok, woc = nextws()
                    wload(wo[:, :].rearrange("p (kk d) -> p kk d", kk=2),
                          wout[h * 256:(h + 1) * 256, :].rearrange("(kk p) d -> p kk d", p=128), wok, woc)
                    hwo[h] = (wo, wok)
                    dma('sp', gngb[h % 2][:, :], gng[0:1, h * 256:(h + 1) * 256].partition_broadcast(128), [],
                        [f"gngb{h % 2}"], f't_gng{h % 2}')

                def proj_qk(h, tgi, c0, n, which):
                    wh3, wr = heads[h]
                    zp = {}
                    for nm_, cs in ((("q", 0), ("qs", 768)) if which == "q" else (("k", 128), ("ks", 896))):
                        pb, pk = nextpb()
                        for k in range(KD):
                            mm(pb[:, 0:n], wh3[:, k, cs:cs + 128], hT[:, k, c0:c0 + n], k == 0, k == KD - 1,
                               wr + [hk(tgi)], [pk])
                        zp[nm_] = (pb, pk)
                    return zp

                def rot(zp, a, b_, dst, dkey, c0, n):
                    tt('dve', t1[:, 0:n], zp[a][0][:, 0:n], cosT[:, c0:c0 + n], ALU.mult, [zp[a][1], "cosT"], ["t1"])
                    tt('dve', t2[:, 0:n], zp[b_][0][:, 0:n], sinT[:, c0:c0 + n], ALU.mult, [zp[b_][1], "sinT"], ["t2"])
                    tt('dve', dst, t1[:, 0:n], t2[:, 0:n], ALU.add, ["t1", "t2"], [dkey])

                def stepsA1(h, tgi, ui):
                    u = ui % 2
                    g3 = ui % 3
                    wh3, wr = heads[h]
                    c0, n = tgs[tgi]

                    def s0():
                        zp = proj_qk(h, tgi, c0, n, "q")
                        rot(zp, "q", "qs", qT[u][:, :], f"qT{u}", c0, n)

                    def s1():
                        zp = proj_qk(h, tgi, c0, n, "k")
                        rot(zp, "k", "ks", kT[u][:, :], f"kT{u}", c0, n)
                        tt('dve', qx[u][:, :].rearrange("p (c n) -> p c n", c=4), qT[u][:, :].rearrange("p (c n) -> p c n", c=4),
                           xitab[:, h:h + 1, :].to_broadcast([128, 4, 128]), ALU.mult, [f"qT{u}", "xitab"], [f"qx{u}"])

                    def mk(cc):
                        def s():
                            cs_ = c0 + cc * 128
                            pv, pvk = nextpb()
                            for k in range(KD):
                                mm(pv[:, 0:256], hT[:, k, cs_:cs_ + 128], wh3[:, k, 256:512], k == 0, k == KD - 1,
                                   wr + [hk(tgi)], [pvk])
                            cp('act', v4[u][:, cc, :], pv[:, 0:256], [pvk], [f"v4{u}"])
                            pg, pgk = nextpb()
                            for k in range(KD):
                                mm(pg[:, 0:256], hT[:, k, cs_:cs_ + 128], wh3[:, k, 512:768], k == 0, k == KD - 1,
                                   wr + [hk(tgi)], [pgk])
                            act(gs4[g3][:, cc, :], pg[:, 0:256], AF.Silu, [pgk], [f"gs4{g3}"])
                        return s
                    return [s0, s1] + [mk(cc) for cc in range(4)]

                def stepsA2(h, tgi, ui):
                    u = ui % 2

                    def t0():
                        pi_, pik = nextpb()
                        for cc in range(4):
                            lc = cc * 128
                            mm(pi_[:, lc:lc + 128], kT[u][:, lc:lc + 128], qT[u][:, lc:lc + 128], True, True,
                               [f"kT{u}", f"qT{u}"], [pik])
                        for cc in range(4):
                            lc = cc * 128
                            tp(PT[:, lc:lc + 128], kT[u][:, lc:lc + 128], identb[:, :], [f"kT{u}", "identb"], ["PT0"])
                        if DM_BCAST:
                            tt('dve', inm4[u][:, :, :], pi_[:, 0:512].rearrange("p (c n) -> p c n", c=4),
                               dmask[:, h:h + 1, :].to_broadcast([128, 4, 128]), ALU.mult, [pik, "dmask"], ["inm4"])
                        else:
                            for cc in range(4):
                                tt('dve', inm4[u][:, cc, :], pi_[:, cc * 128:(cc + 1) * 128], dmask[:, h, :], ALU.mult,
                                   [pik, "dmask"], ["inm4"])
                        ts('dve', kz4[u][:, :, :].rearrange("p c n -> p (c n)"), PT[:, 0:512], zeta[:, h:h + 1], None,
                           ALU.mult, None, ["PT0", "zeta"], ["kz4"])

                    def t1_():
                        for cc in range(4):
                            mm(PK[:, cc * 256:(cc + 1) * 256], kz4[u][:, cc, :], v4[u][:, cc, :], True, True,
                               ["kz4", f"v4{u}"], [("PK", cc // 2)])

                    def t2_():
                        for cc in range(4):
                            stt(S_f[:, h, :], S_f[:, h, :], decay[h], PK[:, cc * 256:(cc + 1) * 256], ALU.mult, ALU.add,
                                [("S_f", h), ("PK", cc // 2)], [("S_f", h)])
                            if cc < 3:
                                cp('act', Sb4[u][:, cc + 1, :], S_f[:, h, :], [("S_f", h)], [("Sb4", cc + 1)])

                    def t3():
                        for cc in range(4):
                            lc = cc * 128
                            pok = ("PO", cc // 2)
                            mm(PO[:, cc * 256:(cc + 1) * 256], inm4[u][:, cc, :], v4[u][:, cc, :], True, False,
                               ["inm4", f"v4{u}"], [pok])
                            if cc == 0:
                                mm(PO[:, 0:256], qx[u][:, lc:lc + 128], S_b[:, h, :], False, True, [f"qx{u}", ("S_b", h)], [pok])
                            else:
                                mm(PO[:, cc * 256:(cc + 1) * 256], qx[u][:, lc:lc + 128], Sb4[u][:, cc, :], False, True,
                                   [f"qx{u}", ("Sb4", cc)], [pok])
                            cp('act', osb4[u][:, cc, :], PO[:, cc * 256:(cc + 1) * 256], [pok], [f"osb4{u}"])
                        cp('act', S_b[:, h, :], S_f[:, h, :], [("S_f", h)], [("S_b", h)])
                    return [t0, t1_, t2_, t3]

                def stepsB(h, tgi, ui):
                    u = ui % 2
                    g3 = ui % 3
                    c0, n = tgs[tgi]
                    gb = gngb[h % 2]
                    gbk = f"gngb{h % 2}"

                    def b0():
                        for cc in range(4):
                            bnstats(bst4[u][:, cc, :], osb4[u][:, cc, :], [f"osb4{u}"], [f"bst4{u}"])
                        for cc in range(4):
                            bnaggr(mv4[u][:, cc, :], bst4[u][:, cc, :], [f"bst4{u}"], [f"mv4{u}"])
                        rstd_pool(rsv4[u][:, :], sdv4[u][:, :], mv4[u][:, :, 1], LN_EPS, [f"mv4{u}"], f"sdv4{u}", [f"rsv4{u}"])
                        for cc in range(4):
                            ts('dve', osb4[u][:, cc, :], osb4[u][:, cc, :], mv4[u][:, cc, 0:1], rsv4[u][:, cc:cc + 1],
                               ALU.subtract, ALU.mult, [f"osb4{u}", f"mv4{u}", f"rsv4{u}"], [f"osb4{u}"])
                        tt('dve', osb4[u][:, :, :], osb4[u][:, :, :],
                           gb[:, :].rearrange("p (o n) -> p o n", o=1).to_broadcast([128, 4, 256]), ALU.mult,
                           [f"osb4{u}", gbk], [f"osb4{u}"])
                        tt('dve', rr4[u][:, :, :], osb4[u][:, :, :], gs4[g3][:, :, :], ALU.mult,
                           [f"osb4{u}", f"gs4{g3}"], [f"rr4{u}"])

                    def b1():
                        for cc in range(4):
                            for e2 in range(2):
                                tp(PT[:, e2 * 512 + cc * 128:e2 * 512 + (cc + 1) * 128], rr4[u][:, cc, e2 * 128:(e2 + 1) * 128],
                                   identb[:, :], [f"rr4{u}", "identb"], ["PT0"])
                        cp('act', rT[u][:, :, :].rearrange("p e t -> p (e t)"), PT[:, 0:1024], ["PT0"], [f"rT{u}"])

                    def b2():
                        wo, wok = hwo[h]
                        out_proj(wo, wok, rT[u], [f"rT{u}"], tgi, c0, n)
                    return [b0, b1, b2]

                def gn_gate_s(o_ps, okey, h):
                    gb = gngb[h % 2]
                    gbk = f"gngb{h % 2}"
                    cp('act', osb_s[:, :], o_ps, [okey], ["osb_s"])
                    bnstats(bst_s[:, :], osb_s[:, :], ["osb_s"], ["bst_s"])
                    bnaggr(mv_s[:, :], bst_s[:, :], ["bst_s"], ["mv_s"])
                    rstd_pool(rsv_s[:, :], sdv_s[:, :], mv_s[:, 1:2], LN_EPS, ["mv_s"], "sdv_s", ["rsv_s"])
                    ts('dve', osb_s[:, :], osb_s[:, :], mv_s[:, 0:1], rsv_s[:, 0:1], ALU.subtract, ALU.mult,
                       ["osb_s", "mv_s", "rsv_s"], ["osb_s"])
                    tt('dve', osb_s[:, :], osb_s[:, :], gb[0:NS, :], ALU.mult, ["osb_s", gbk], ["osb_s"])
                    tt('dve', rr_s[:, :], osb_s[:, :], gs_s[:, :], ALU.mult, ["osb_s", "gs_s"], ["rr_s"])

                def sample_head(h):
                    wh3, wr = heads[h]
                    wo, wok = hwo[h]
                    tgi = 2
                    c0, n = tgs[tgi]
                    zp = proj_qk(h, tgi, c0, n, "q")
                    rot(zp, "q", "qs", qsf[:, :], "qsf", c0, n)
                    zp = proj_qk(h, tgi, c0, n, "k")
                    rot(zp, "k", "ks", ksf[:, :], "ksf", c0, n)
                    cp('dve', qm[:, :].rearrange("p (a c) -> p a c", c=17)[:, :, 0], qsf[:, :], ["qsf"], ["qm"])
                    pb, pk = nextpb()
                    tp(pb[0:NS, 0:128], ksf[:, :], identf[:, :], ["ksf", "identf"], [pk])
                    ts('dve', kTM[:, :], pb[0:NS, 0:128], float(128 ** -0.5), None, ALU.mult, None, [pk], ["kTM"])
                    pv, pvk = nextpb()
                    for k in range(KD):
                        mm(pv[0:NS, 0:512], hT[:, k, c0:c0 + n], wh3[:, k, 256:768], k == 0, k == KD - 1,
                           wr + [hk(tgi)], [pvk])
                    cp('act', vsf[:, :], pv[0:NS, 0:256], [pvk], ["vsf"])
                    act(gs_s[:, :], pv[0:NS, 256:512], AF.Silu, [pvk], ["gs_s"])
                    pok = ("PO", 0)
                    for b in range(4):
                        dma('sp', so[b][:, :], sret[b, h], [], [f"so{b}"], f'so{b}')
                    for b0 in range(0, NS, 4):
                        for bb in range(4):
                            b = b0 + bb
                            ts('dve', km4[:, bb, :], kTM[:, :], identf[0:NS, b:b + 1], None, ALU.mult, None,
                               ["kTM", "identf"], ["km4"])
                        for bb in range(4):
                            mm(PK[:, bb * 256:(bb + 1) * 256], km4[:, bb, :], vsf[:, :], True, True, ["km4", "vsf"],
                               [("PK", bb // 2)])
                        for bb in range(4):
                            b = b0 + bb
                            stt(sn[bb][:, :], so[bb][:, :], gam[h], PK[:, bb * 256:(bb + 1) * 256], ALU.mult, ALU.add,
                                [f"so{bb}", ("PK", bb // 2)], [f"sn{bb}"])
                            dma('sp', rss[b, h], sn[bb][:, :], [f"sn{bb}"], [], f'o_sn{bb}')
                            if b + 4 < NS:
                                dma('sp', so[bb][:, :], sret[b + 4, h], [], [f"so{bb}"], f'so{bb}')
                        for bb in range(4):
                            b = b0 + bb
                            mm(PO[0:NS, 0:256], qm[:, b * 16:(b + 1) * 16], sn[bb][:, :], b == 0, b == NS - 1,
                               ["qm", f"sn{bb}"], [pok])
                    gn_gate_s(PO[0:NS, 0:256], pok, h)
                    pt_, ptk = nextpt()
                    for e2 in range(2):
                        tp(pt_[:, e2 * NS:(e2 + 1) * NS], rr_s[:, e2 * 128:(e2 + 1) * 128], identb[0:NS, 0:NS],
                           ["rr_s", "identb"], [ptk])
                    cp('act', rT_s[:, :, :], pt_[:, 0:2 * NS].rearrange("p (e t) -> p e t", e=2), [ptk], ["rT_s"])
                    out_proj(wo, wok, rT_s, ["rT_s"], tgi, c0, n)

                units = [(h, tgi) for h in range(4) for tgi in range(2)]
                NU = len(units)
                load_wh(0)
                load_wo(0)
                order = [("A2", 0), ("A1", 0), ("A2", 1), ("A1", 1), ("A2", 2), ("B", 0), ("A1", 2), ("A1", 3), ("A2", 3),
                         ("A1", 4), ("B", 1), ("A1", 5), ("B", 2)]
                for it in range(NU + 2):
                    if it % 2 == 0:
                        h_ = it // 2
                        if h_ + 1 < 4:
                            load_wh(h_ + 1)
                        if 1 <= h_ < 4:
                            load_wo(h_)
                    st_ = {}
                    if it < NU:
                        st_["A1"] = stepsA1(units[it][0], units[it][1], it)
                    if 0 <= it - 1 < NU:
                        st_["A2"] = stepsA2(units[it - 1][0], units[it - 1][1], it - 1)
                    if 0 <= it - 2 < NU:
                        st_["B"] = stepsB(units[it - 2][0], units[it - 2][1], it - 2)
                    for nm_, si in order:
                        if nm_ in st_:
                            st_[nm_][si]()
                    if 0 <= it - 2 < NU and units[it - 2][1] == 1:
                        hd = units[it - 2][0]
                        if last:
                            sample_head(hd)
                        if last:
                            dma('sp', rsp[hd], S_f[:, hd, :], [("S_f", hd)], [], 'o_Sf')
                S.barrier()

            with ExitStack() as ph:
                def T(name, shape, dt=F32):
                    return ph.enter_context(nc.sbuf_tensor(f"sb_{name}_g{p}", list(shape), dt))
                slgb = T("slgb", [128, D]); slbb = T("slbb", [128, D])
                ug = [T(f"ug{i}", [128, 2, 512]) for i in range(2)]
                sg4 = [T(f"sg4{i}", [128, 4, 256]) for i in range(2)]
                snb4 = [T(f"snb4{i}", [128, 4, 256], BF16) for i in range(2)]
                bst4 = [T(f"bst4{i}", [128, 4, 6]) for i in range(2)]
                mv4 = [T(f"mv4{i}", [128, 4, 2]) for i in range(2)]
                sdv4 = [T(f"sdv4{i}", [128, 4]) for i in range(2)]
                rsv4 = [T(f"rsv4{i}", [128, 4]) for i in range(2)]
                goT = [T(f"goT{i}", [128, 2, 512], BF16) for i in range(2)]
                mb = T("mb", [128, 512])
                dma('sp', slgb[:, :], slg[0:1, :].partition_broadcast(128), [], ["slgb"], 't_slg')
                dma('sp', slbb[:, :], slb[0:1, :].partition_broadcast(128), [], ["slbb"], 't_slb')
                win3 = win.rearrange("(k p) c -> p k c", p=128)
                groups = {}

                def load_group(g):
                    wg_, wgk, wgc = nextwb()
                    wo, wok, woc = nextws()
                    wg3 = wg_[:, 0:KD * 512].rearrange("p (k c) -> p k c", k=KD)
                    wload(wg3[:, :, 0:256], win3[:, :, 3072 + g * 256:3072 + (g + 1) * 256], wgk, wgc)
                    wload(wg3[:, :, 256:512], win3[:, :, 4096 + g * 256:4096 + (g + 1) * 256], wgk, wgc)
                    wload(wo[:, :].rearrange("p (kk d) -> p kk d", kk=2),
                          wout[1024 + g * 256:1024 + (g + 1) * 256, :].rearrange("(kk p) d -> p kk d", p=128), wok, woc)
                    groups[g] = (wg3, wgk, wo, wok)

                def sguA(g, tgi, u):
                    wg3, wgk, wo, wok = groups[g]
                    c0, n = tgs[tgi]
                    samp = (n == NS)
                    for dd in range(2):
                        pb, pk = nextpb()
                        for k in range(KD):
                            mm(pb[:, 0:n], wg3[:, k, dd * 128:(dd + 1) * 128], hT[:, k, c0:c0 + n], k == 0, k == KD - 1,
                               [wgk, hk(tgi)], [pk])
                        act(ug[u][:, dd, 0:n], pb[:, 0:n], AF.Gelu, [pk], [f"ug{u}"])
                    nch = 1 if samp else 4
                    np_ = NS if samp else 128
                    ncol = NS if samp else 128
                    for cc in range(nch):
                        cs_ = c0 + cc * 128
                        pz, pzk = nextpb()
                        for k in range(KD):
                            mm(pz[0:np_, 0:256], hT[:, k, cs_:cs_ + ncol], wg3[:, k, 256:512], k == 0, k == KD - 1,
                               [wgk, hk(tgi)], [pzk])
                        act(sg4[u][0:np_, cc, :], pz[0:np_, 0:256], AF.Gelu, [pzk], [f"sg4{u}"])

                def sguB(g, tgi, u):
                    wg3, wgk, wo, wok = groups[g]
                    c0, n = tgs[tgi]
                    samp = (n == NS)
                    nch = 1 if samp else 4
                    np_ = NS if samp else 128
                    for cc in range(nch):
                        bnstats(bst4[u][0:np_, cc, :], sg4[u][0:np_, cc, :], [f"sg4{u}"], [f"bst4{u}"])
                    for cc in range(nch):
                        bnaggr(mv4[u][0:np_, cc, :], bst4[u][0:np_, cc, :], [f"bst4{u}"], [f"mv4{u}"])
                    rstd_pool(rsv4[u][0:np_, 0:nch], sdv4[u][0:np_, 0:nch], mv4[u][0:np_, 0:nch, 1], LN_EPS,
                              [f"mv4{u}"], f"sdv4{u}", [f"rsv4{u}"])
                    for cc in range(nch):
                        ts('dve', sg4[u][0:np_, cc, :], sg4[u][0:np_, cc, :], mv4[u][0:np_, cc, 0:1], rsv4[u][0:np_, cc:cc + 1],
                           ALU.subtract, ALU.mult, [f"sg4{u}", f"mv4{u}", f"rsv4{u}"], [f"sg4{u}"])
                    tt('dve', sg4[u][0:np_, 0:nch, :], sg4[u][0:np_, 0:nch, :],
                       slgb[0:np_, g * 256:(g + 1) * 256].rearrange("p (o n) -> p o n", o=1).to_broadcast([np_, nch, 256]),
                       ALU.mult, [f"sg4{u}", "slgb"], [f"sg4{u}"])
                    if samp:
                        tt('dve', sg4[u][0:NS, 0, :], sg4[u][0:NS, 0, :], slbb[0:NS, g * 256:(g + 1) * 256], ALU.add,
                           [f"sg4{u}", "slbb"], [f"sg4{u}"])
                        dma('sp', sgur[:, g * 256:(g + 1) * 256], sg4[u][0:NS, 0, :], [f"sg4{u}"], [], f'o_sg4{u}')
                        cp('dve', snb4[u][0:NS, 0, :], sg4[u][0:NS, 0, :], [f"sg4{u}"], [f"snb4{u}"])
                    else:
                        tt('dve', snb4[u][:, :, :], sg4[u][:, :, :],
                           slbb[:, g * 256:(g + 1) * 256].rearrange("p (o n) -> p o n", o=1).to_broadcast([128, 4, 256]),
                           ALU.add, [f"sg4{u}", "slbb"], [f"snb4{u}"])
                    for cc in range(nch):
                        lc = cc * 128
                        for dd in range(2):
                            plk = ("PO", dd)
                            pl = PO[:, dd * 512:(dd + 1) * 512]
                            if samp:
                                mm(pl[:, 0:NS], snb4[u][0:NS, 0, dd * 128:(dd + 1) * 128], Wdiag[:, g * 16:(g + 1) * 16],
                                   True, True, [f"snb4{u}", "Wdiag"], [plk])
                            else:
                                mm(pl[:, lc:lc + 128], snb4[u][:, cc, dd * 128:(dd + 1) * 128], WsT[:, g, :], True, True,
                                   [f"snb4{u}", "WsT"], [plk])
                    for dd in range(2):
                        if samp:
                            tt('dve', mb[:, 0:n], PO[:, dd * 512:dd * 512 + n], biasb[:, g, 0:1].to_broadcast([128, NS]),
                               ALU.add, [("PO", dd), "biasb"], ["mb"])
                        else:
                            tt('dve', mb[:, :].rearrange("p (c i) -> p c i", c=4),
                               PO[:, dd * 512:(dd + 1) * 512].rearrange("p (c i) -> p c i", c=4),
                               biasb[:, g:g + 1, :].to_broadcast([128, 4, 128]), ALU.add, [("PO", dd), "biasb"], ["mb"])
                        tt('dve', goT[u][:, dd, 0:n], mb[:, 0:n], ug[u][:, dd, 0:n], ALU.mult,
                           ["mb", f"ug{u}"], [f"goT{u}"])
                    out_proj(wo, wok, goT[u], [f"goT{u}"], tgi, c0, n)

                units = [(g, tgi) for g in range(4) for tgi in range(len(tgs))]
                load_group(0)
                prev = None
                for ui, (g, tgi) in enumerate(units):
                    sguA(g, tgi, ui % 2)
                    if prev is not None:
                        sguB(*prev)
                    if tgi == 0 and g + 1 < 4:
                        load_group(g + 1)
                    prev = (g, tgi, ui % 2)
                sguB(*prev)
                pre['ffn0'] = ffn_load(0, 0)
                dump(p * 4 + 0)
                S.barrier()

            with ExitStack() as ph:
                rmsnorm(VR['nf0'], ph, f"b{p}")
                ffn(0, ph, f"a{p}", pre0=pre.pop('ffn0', None), hook3=conv_load_ab)
                dump(p * 4 + 1)
                S.barrier()


            with ExitStack() as ph:
                def T(name, shape, dt=F32):
                    return ph.enter_context(nc.sbuf_tensor(f"sb_{name}_c{p}", list(shape), dt))
                xgx = T("xgx", [128, KD, 30 + P], BF16)
                sig = [T(f"sig{i}", [128, 512]) for i in range(2)]
                xgs = T("xgs", [128, KD, NS])
                xgtail = T("xgtail", [128, KD, 32])
                dg = [T(f"dg{i}", [128, 128], BF16) for i in range(4)]
                convT = T("convT", [128, KD, 512])
                convTb = hT.bitcast(F32)
                hkeys = [hk(t_) for t_ in range(3)]
                sqc = T("sqc", [128, KD, 512], BF16)
                cn = sqc
                mean = T("mean", [128, 512]); ex2 = T("ex2", [128, 512])
                u1 = [T(f"u1{i}", [128, 512]) for i in range(2)]
                onesF = T("onesF", [128, 128])
                memset(onesF[:, :], 1.0, ["onesF"])
                if last:
                    hs = [T("hs0", [NS, D])]
                    tl = T("tl", [32, D])

                rmsnorm(VR['nm1'], ph, f"c{p}", ext=(sqc, mean, ex2, ["sqc"] + [("cn", k) for k in range(KD)], "mean", "ex2"))
                cp('dve', xgx[:, :, 0:30], hist[:, :, :], ["hist"], [("xgx", k, -1) for k in range(KD)])
                if 'conv' not in pre:
                    conv_load_ab()
                wa3, wak, wb3, wbk = pre.pop('conv')
                wc, wck, wcc = nextwb()
                wc3 = wc[:, :].rearrange("p (k c) -> p k c", k=KD)
                for hh in range(2):
                    wload(wc3[:, hh * 4:hh * 4 + 4, :], cwout.rearrange("(k p) c -> p k c", p=128)[:, hh * 4:hh * 4 + 4, :], wck, wcc)
                for tgi, (c0, n) in enumerate(tgs):
                    samp = (n == NS)
                    for cc in range(KD):
                        i = cc % 2
                        pa, pak = nextpb()
                        for k in range(KD):
                            mm(pa[:, 0:n], wa3[:, k, cc * 128:(cc + 1) * 128], hT[:, k, c0:c0 + n], k == 0, k == KD - 1,
                               [wak, hk(tgi)], [pak])
                        pg, pgk = nextpb()
                        for k in range(KD):
                            mm(pg[:, 0:n], wb3[:, k, cc * 128:(cc + 1) * 128], hT[:, k, c0:c0 + n], k == 0, k == KD - 1,
                               [wbk, hk(tgi)], [pgk])
                        act(sig[i][:, 0:n], pg[:, 0:n], AF.Sigmoid, [pgk, "vecT"], [f"sig{i}"],
                            bias=vecT[:, cc, VR['cbg']:VR['cbg'] + 1], scale=1.0)
                        if samp:
                            stt(xgs[:, cc, :], pa[:, 0:n], vecT[:, cc, VR['cba']:VR['cba'] + 1], sig[i][:, 0:n],
                                ALU.add, ALU.mult, [pak, "vecT", f"sig{i}"], [("xgs", cc)])
                        else:
                            stt(xgx[:, cc, 30 + c0:30 + c0 + n], pa[:, 0:n], vecT[:, cc, VR['cba']:VR['cba'] + 1],
                                sig[i][:, 0:n], ALU.add, ALU.mult, [pak, "vecT", f"sig{i}"], [("xgx", cc, tgi)])
                            if last and tgi == 1:
                                stt(xgtail[:, cc, :], pa[:, n - 32:n], vecT[:, cc, VR['cba']:VR['cba'] + 1],
                                    sig[i][:, n - 32:n], ALU.add, ALU.mult, [pak, "vecT", f"sig{i}"], [("xgt", cc)])
                if not last:
                    cp('dve', hist[:, :, :], xgx[:, :, P:P + 30], [("xgx", k, 1) for k in range(KD)], ["hist"])
                pre['ffn1'] = ffn_load(1, 0)
                ptg = [(tgi, c0, n) for tgi, (c0, n) in enumerate(tgs) if n != NS]
                convT2 = [convT, convTb]
                for cc in range(KD):
                    banks = [nextpb() for _ in ptg]
                    for j in range(CW):
                        di = st.setdefault('dg', 0) % 4
                        st['dg'] += 1
                        ts('dve', dg[di][:, :], identb[:, :], vecT[:, cc, VR['dww'] + j:VR['dww'] + j + 1], None,
                           ALU.mult, None, ["identb", "vecT"], [f"dg{di}"])
                        for (tgi, c0, n), (pb, pk) in zip(ptg, banks):
                            mm(pb[:, 0:n], dg[di][:, :], xgx[:, cc, c0 + j:c0 + j + n], j == 0, j == CW - 1,
                               [f"dg{di}", ("xgx", cc, tgi), ("xgx", cc, tgi - 1)], [pk])
                    for (tgi, c0, n), (pb, pk) in zip(ptg, banks):
                        act(convT2[tgi][:, cc, 0:n], pb[:, 0:n], AF.Identity, [pk, "vecT"],
                            [("convT", tgi % 2, cc)] + (hkeys if tgi % 2 else []),
                            bias=vecT[:, cc, VR['dwb']:VR['dwb'] + 1], scale=1.0)
                for tgi, (c0, n) in enumerate(tgs):
                    samp = (n == NS)
                    convT = convT2[tgi % 2]
                    if not samp:
                        pass
                    else:
                        for cc in range(KD):
                            stt(u1[0][:, cc * NS:(cc + 1) * NS], xgs[:, cc, :], vecT[:, cc, VR['dww'] + 30:VR['dww'] + 31],
                                hconv[:, cc * NS:(cc + 1) * NS], ALU.mult, ALU.add, [("xgs", cc), "vecT", "hconv"], ["u10"])
                            ts('dve', convT[:, cc, 0:n], u1[0][:, cc * NS:(cc + 1) * NS], vecT[:, cc, VR['dwb']:VR['dwb'] + 1],
                               None, ALU.add, None, ["u10", "vecT"], [("convT", tgi % 2, cc)])
                    ck = [("convT", tgi % 2, cc) for cc in range(KD)] + (hkeys if tgi % 2 else [])
                    act(sqc[:, :, 0:n], convT[:, :, 0:n], AF.Square, ck, ["sqc"] + [("cn", k) for k in range(KD)])
                    pm, pmk = nextpb()
                    for k in range(KD):
                        mm(pm[:, 0:n], onesF[:, :], convT[:, k, 0:n], k == 0, k == KD - 1, ck + ["onesF"], [pmk])
                    pq, pqk = nextpb()
                    for k in range(KD):
                        mm(pq[:, 0:n], onesb[:, :], sqc[:, k, 0:n], k == 0, k == KD - 1, ["sqc", "onesb"], [pqk])
                    ts('dve', mean[:, 0:n], pm[:, 0:n], 1.0 / D, None, ALU.mult, None, [pmk], ["mean"])
                    ts('dve', ex2[:, 0:n], pq[:, 0:n], 1.0 / D, None, ALU.mult, None, [pqk], ["ex2"])
                    tt('dve', u1[0][:, 0:n], mean[:, 0:n], mean[:, 0:n], ALU.mult, ["mean"], ["u10"])
                    tt('dve', ex2[:, 0:n], ex2[:, 0:n], u1[0][:, 0:n], ALU.subtract, ["ex2", "u10"], ["ex2"])
                    act(ex2[:, 0:n], ex2[:, 0:n], AF.Sqrt, ["ex2"], ["ex2"], bias=float(LN_EPS), scale=1.0)
                    recip(ex2[:, 0:n], ex2[:, 0:n], ["ex2"], ["ex2"])
                    tt('dve', mean[:, 0:n], mean[:, 0:n], ex2[:, 0:n], ALU.mult, ["mean", "ex2"], ["mean"])
                    for cc in range(KD):
                        i = cc % 2
                        tt('dve', u1[i][:, 0:n], convT[:, cc, 0:n], ex2[:, 0:n], ALU.mult, [("convT", tgi % 2, cc), "ex2"] + (hkeys if tgi % 2 else []), [f"u1{i}"])
                        tt('dve', u1[i][:, 0:n], u1[i][:, 0:n], mean[:, 0:n], ALU.subtract, [f"u1{i}", "mean"], [f"u1{i}"])
                        act(cn[:, cc, 0:n], u1[i][:, 0:n], AF.Silu, [f"u1{i}", "vecT"], [("cn", cc), "sqc"],
                            bias=vecT[:, cc, VR['clb']:VR['clb'] + 1], scale=vecT[:, cc, VR['clg']:VR['clg'] + 1])
                    for d in range(KD):
                        pb, pk = nextpb()
                        for k in range(KD):
                            mm(pb[:, 0:n], wc3[:, k, d * 128:(d + 1) * 128], cn[:, k, 0:n], k == 0, k == KD - 1,
                               [wck, ("cn", k)], [pk])
                        stt(xT[:, d, c0:c0 + n], pb[:, 0:n], vecT[:, d, VR['cbo']:VR['cbo'] + 1], xT[:, d, c0:c0 + n],
                            ALU.add, ALU.add, [pk, "vecT", xk(d, tgi)], [xk(d, tgi)])
                if last:
                    for half in range(2):
                        pb, pk = nextpb()
                        for kk in range(4):
                            k = half * 4 + kk
                            tp(pb[0:32, kk * 128:(kk + 1) * 128], xgtail[:, k, :], identf[:, :], [("xgt", k), "identf"], [pk])
                        cp('act', tl[:, half * 512:(half + 1) * 512], pb[0:32, :], [pk], [f"tl{half}"])
                    dma('sp', csp[:, :], tl[2:32, :], ["tl0", "tl1"], [], 'o_tl')
                    for half in range(2):
                        pb, pk = nextpb()
                        for kk in range(4):
                            k = half * 4 + kk
                            tp(pb[0:NS, kk * 128:(kk + 1) * 128], xgs[:, k, :], identf[:, :], [("xgs", k), "identf"], [pk])
                        cp('act', hs[0][0:NS, half * 512:(half + 1) * 512], pb[0:NS, :], [pk], [f"hs0"])
                    dma('sp', css.rearrange("(b j) d -> b j d", j=30)[:, 29, :], hs[0][0:NS, :], ["hs0"], [], 'o_hs0')
                dump(p * 4 + 2)
                S.barrier()

            with ExitStack() as ph:
                rmsnorm(VR['nf1'], ph, f"d{p}")
                ffn(1, ph, f"b{p}", pre0=pre.pop('ffn1', None))
                dump(p * 4 + 3)
                S.barrier()

            with ExitStack() as ph:
                sq = ph.enter_context(nc.sbuf_tensor(f"sqf{p}", [128, KD, 512], BF16))
                sd = ph.enter_context(nc.sbuf_tensor(f"sdf{p}", [128, 512], F32))
                rs = ph.enter_context(nc.sbuf_tensor(f"rsf{p}", [128, 512], F32))
                yT = ph.enter_context(nc.sbuf_tensor(f"yT{p}", [128, KD, 512], F32))
                yo = [ph.enter_context(nc.sbuf_tensor(f"yo{p}{i}", [128, D], F32)) for i in range(2)]
                xin_next = load_x_pre(p + 1, ph) if p + 1 < NPASS else None
                for tgi, (c0, n) in enumerate(tgs):
                    act(sq[:, :, 0:n], xT[:, :, c0:c0 + n], AF.Square, [xk(k, tgi) for k in range(KD)], ["sq"])
                    pb, pk = nextpb()
                    for k in range(KD):
                        mm(pb[:, 0:n], onesb[:, :], sq[:, k, 0:n], k == 0, k == KD - 1, ["sq", "onesb"], [pk])
                    act(sd[:, 0:n], pb[:, 0:n], AF.Sqrt, [pk], ["sd"], bias=float(RMS_EPS), scale=1.0 / D)
                    recip(rs[:, 0:n], sd[:, 0:n], ["sd"], ["rs"])
                    for k in range(KD):
                        stt(yT[:, k, 0:n], xT[:, k, c0:c0 + n], vecT[:, k, VR['fin']:VR['fin'] + 1], rs[:, 0:n],
                            ALU.mult, ALU.mult, [xk(k, tgi), "vecT", "rs"], [("yT", k)])
                    nch = 1 if n == NS else 4
                    np_ = NS if n == NS else 128
                    for cc in range(nch):
                        oi = st.setdefault('yo', 0) % 2
                        st['yo'] += 1
                        for half in range(2):
                            pb, pk = nextpb()
                            for kk in range(4):
                                k = half * 4 + kk
                                tp(pb[0:np_, kk * 128:(kk + 1) * 128], yT[:, k, cc * 128:cc * 128 + np_], identf[:, :],
                                   [("yT", k), "identf"], [pk])
                            cp('act' if half == 0 else 'dve', yo[oi][0:np_, half * 512:(half + 1) * 512], pb[0:np_, :],
                               [pk], [f"yo{oi}{half}"])
                        if n == NS:
                            dma('sp', ys[:, :], yo[oi][0:NS, :], [f"yo{oi}0", f"yo{oi}1"], [], f'o_yo{oi}')
                        else:
                            r0 = p * P + c0 + cc * 128
                            dma('sp', yp[r0:r0 + 128, :], yo[oi][:, :], [f"yo{oi}0", f"yo{oi}1"], [], f'o_yo{oi}')
                if p + 1 < NPASS:
                    load_x(p + 1, ph, xin_next)
                S.barrier()

        chan_sems = {c: es.enter_context(nc.semaphore("c_" + c)) for c in sorted(S.chan_n)}
        print("ops", len(S.ops), "chans", len(chan_sems), flush=True)
        with nc.Block() as block:
            S.emit(nc, block, sems, chan_sems, wait_all_chans={'const'},
                   final_chans=[c for c in sorted(S.chan_n) if c.startswith('o_')])
    return nc, consts


_CACHE = {}


def kernel(x_prompt, x_sample, state_ret, state_conv, norm_mix_g, norm_ffn_g, final_norm_g,
           ab_w_in, ab_w_out, ret_gn_g, sgu_ln_g, sgu_ln_b, sgu_w, sgu_b,
           conv_w_in, conv_b_in, conv_dw_w, conv_dw_b, conv_ln_g, conv_ln_b,
           conv_w_out, conv_b_out, ffn_w1, ffn_w2):
    f = lambda a: np.ascontiguousarray(np.asarray(a, dtype=np.float32))
    if 'nc' not in _CACHE:
        _CACHE['nc'] = build_nc()
    nc, consts = _CACHE['nc']
    vecs = np.concatenate([
        f(norm_mix_g)[0:1], f(norm_mix_g)[1:2], f(norm_ffn_g)[0:1], f(norm_ffn_g)[1:2], f(final_norm_g)[None, :],
        f(conv_b_in)[0:1, 0:1024], f(conv_b_in)[0:1, 1024:2048], f(conv_dw_b)[0:1], f(conv_ln_g)[0:1],
        f(conv_ln_b)[0:1], f(conv_b_out)[0:1], f(conv_dw_w)[0]], axis=0)
    assert vecs.shape == (NVEC, D)
    shared = dict(
        win=f(ab_w_in)[0], wout=f(ab_w_out)[0], gng=f(ret_gn_g), slg=f(sgu_ln_g), slb=f(sgu_ln_b),
        sguw=f(sgu_w)[0], sgub=f(sgu_b)[0].reshape(1, 512), cwin=f(conv_w_in)[0], cwout=f(conv_w_out)[0],
        w1=f(ffn_w1), w2=f(ffn_w2), vecs=f(vecs),
        dmask=consts['dmask'].reshape(128, 512), zeta=consts['zeta'], xitab=consts['xitab'].reshape(128, 512),
        ropec=consts['ropec'], ropes=consts['ropes'], ident=consts['ident'], tri=consts['tri'], ind=consts['ind'])
    xp_ = f(x_prompt); xs_ = f(x_sample); sr = f(state_ret); scv = f(state_conv)
    in_maps = []
    for c in range(8):
        m = dict(shared)
        m['xp'] = xp_[c]
        m['xs'] = xs_[c * NS:(c + 1) * NS, 0, :]
        m['sret'] = sr[0, c * NS:(c + 1) * NS]
        m['sconv'] = scv[0, c * NS:(c + 1) * NS].reshape(NS * 30, D)
        in_maps.append(m)
    res = run_bass_kernel_spmd(nc, in_maps, core_ids=list(range(8)))
    R = res.results
    _CACHE['R'] = R
    y_prompt = np.stack([R[c]['yp'] for c in range(8)], axis=0).astype(np.float32)
    y_sample = np.concatenate([R[c]['ys'] for c in range(8)], axis=0).reshape(128, 1, D).astype(np.float32)
    rsp = np.stack([R[c]['rsp'] for c in range(8)], axis=0)[None].astype(np.float32)
    rss = np.concatenate([R[c]['rss'] for c in range(8)], axis=0)[None].astype(np.float32)
    sgur = np.concatenate([R[c]['sgur'] for c in range(8)], axis=0).reshape(1, 128, 1, D).astype(np.float32)
    csp = np.stack([R[c]['csp'] for c in range(8)], axis=0)[None].astype(np.float32)
    css = np.concatenate([R[c]['css'].reshape(NS, 30, D) for c in range(8)], axis=0)[None].astype(np.float32)
    return (y_prompt, y_sample, rsp, rss, sgur, csp, css)
```

```python
import numpy as np
from contextlib import ExitStack
import concourse.bass as bass
import concourse.mybir as mybir
from concourse.bass_utils import run_bass_kernel_spmd

F32 = mybir.dt.float32
BF16 = mybir.dt.bfloat16
AF = mybir.ActivationFunctionType
ALU = mybir.AluOpType

D = 1024
KD = 8
SEQ = 2048
P = 1024
NPASS = SEQ // P
NS = 16
NT = P + NS
PAST = 16384
RMS_EPS = 1e-6
LN_EPS = 1e-5
NVEC = 42
CW = 31


class Sched:
    def __init__(self):
        self.ops = []
        self.lastw = {}
        self.readers = {}
        self.chan_n = {}
        self.barrier_ops = []
        self.first_after = {}
        self.bar_pos = 0

    def add(self, eng, fn, r=(), w=(), chan=None):
        if STRICT:
            def nk(k):
                if isinstance(k, str) and k.startswith("PT"):
                    return "PT"
                if COARSE_POPK and isinstance(k, tuple) and k[0] in ("PO", "PK"):
                    return k[0]
                return k
            r = [nk(k) for k in r]
            w = [nk(k) for k in w]
        oid = len(self.ops)
        deps = {}

        def dep(o, kind):
            if o is not None:
                deps[o] = deps.get(o, 0) | kind

        for k in r:
            dep(self.lastw.get(k), 1)
        for k in w:
            lw = self.lastw.get(k)
            if not (lw is not None and chan is not None and self.ops[lw]['chan'] == chan):
                dep(lw, 2)
            rd = self.readers.get(k)
            if rd:
                for o in rd.values():
                    dep(o, 4)
        if eng not in self.first_after:
            for o in self.barrier_ops:
                dep(o, 1)
            self.first_after[eng] = oid
        for k in w:
            self.lastw[k] = oid
            self.readers[k] = {}
        ws = set(w)
        for k in r:
            if k not in ws:
                rd = self.readers.setdefault(k, {})
                if chan is not None:
                    rd[('dma', oid)] = oid
                else:
                    rd[eng] = oid
        cnt = None
        if chan is not None:
            self.chan_n[chan] = self.chan_n.get(chan, 0) + 1
            cnt = self.chan_n[chan]
        self.ops.append(dict(eng=eng, fn=fn, deps=deps, chan=chan, cnt=cnt, has_r=bool(r)))
        return oid

    def barrier(self):
        b = []
        seen = set()
        for i in range(len(self.ops) - 1, self.bar_pos - 1, -1):
            o = self.ops[i]
            if o['chan'] is not None:
                if o['has_r']:
                    b.append(i)
            elif o['eng'] not in seen:
                seen.add(o['eng'])
                b.append(i)
        for i in self.barrier_ops:
            o = self.ops[i]
            if o['chan'] is None and o['eng'] not in seen:
                seen.add(o['eng'])
                b.append(i)
        self.barrier_ops = b
        self.first_after = {}
        self.bar_pos = len(self.ops)

    def emit(self, nc, block, sems, chan_sems, wait_all_chans, final_chans):
        ops = self.ops
        n = len(ops)
        sig = [False] * n
        for o in ops:
            for d, kind in o['deps'].items():
                p = ops[d]
                if p['chan'] is not None:
                    continue
                if p['eng'] == o['eng'] and o['chan'] is None:
                    if o['eng'] == 'pe' or (not (kind & 1) and not STRICT_SE):
                        continue
                sig[d] = True
        sigidx = [0] * n
        cnt = {}
        for i, o in enumerate(ops):
            if o['chan'] is None and sig[i]:
                cnt[o['eng']] = cnt.get(o['eng'], 0) + 1
                sigidx[i] = cnt[o['eng']]
        self.n_sig = dict(cnt)
        by_eng = {}
        for i, o in enumerate(ops):
            by_eng.setdefault(o['eng'], []).append(i)

        def run(engname, e):
            waited = {}
            for i in by_eng.get(engname, ()):
                o = ops[i]
                need = {}
                for d, kind in o['deps'].items():
                    p = ops[d]
                    if p['chan'] is not None:
                        c = p['chan']
                        v = 16 * (self.chan_n[c] if c in wait_all_chans else p['cnt'])
                        key = ('c', c)
                    else:
                        if p['eng'] == o['eng'] and o['chan'] is None:
                            if o['eng'] == 'pe' or (not (kind & 1) and not STRICT_SE):
                                continue
                        v = sigidx[d]
                        key = ('e', p['eng'])
                    if v > need.get(key, 0):
                        need[key] = v
                for key, v in need.items():
                    if waited.get(key, 0) >= v:
                        continue
                    waited[key] = v
                    s = chan_sems[key[1]] if key[0] == 'c' else sems[key[1]]
                    e.wait_ge(s, v)
                ins = o['fn'](e)
                if o['chan'] is not None:
                    ins.then_inc(chan_sems[o['chan']], 16)
                elif sig[i]:
                    ins.then_inc(sems[o['eng']], 1)
            if engname == 'sp':
                for c in final_chans:
                    if self.chan_n.get(c, 0):
                        e.wait_ge(chan_sems[c], 16 * self.chan_n[c])

        @block.tensor
        def _(e):
            run('pe', e)

        @block.scalar
        def _(e):
            run('act', e)

        @block.vector
        def _(e):
            run('dve', e)

        @block.gpsimd
        def _(e):
            run('pool', e)

        @block.sync
        def _(e):
            run('sp', e)


def _consts():
    log_g = np.log1p(-(2.0 ** (-5.0 - np.arange(4, dtype=np.float32)))).astype(np.float32)
    idx = np.arange(128, dtype=np.float32)
    sc = np.float32(128 ** -0.5)
    diff = idx[None, :] - idx[:, None]
    dm = np.zeros((128, 4, 128), np.float32)
    for h in range(4):
        dm[:, h, :] = np.where(diff >= 0, np.exp(np.maximum(diff, 0.0) * log_g[h]), 0.0) * sc
    zeta = np.zeros((128, 4), np.float32)
    xi = np.zeros((128, 4, 128), np.float32)
    for h in range(4):
        zeta[:, h] = np.exp((127.0 - idx) * log_g[h]) * sc
        x1 = np.exp((idx + 1.0) * log_g[h])
        xi[:, h, :] = x1[None, :]
    decay = [float(np.exp(np.float32(128.0) * log_g[h])) for h in range(4)]
    gam = [float(np.exp(log_g[h])) for h in range(4)]
    half = 64
    inv = (10000.0 ** (-np.arange(half, dtype=np.float32) / half)).astype(np.float32)
    pos = np.concatenate([np.arange(SEQ), np.full(NS, PAST)]).astype(np.float32)
    ang = (pos[:, None] * inv[None, :]).astype(np.float32)
    c = np.cos(ang).astype(np.float32).T
    s = np.sin(ang).astype(np.float32).T
    ropec = np.concatenate([c, c], axis=0)
    ropes = np.concatenate([-s, s], axis=0)
    ident = np.eye(128, dtype=np.float32)
    tri = (idx[None, :] >= idx[:, None]).astype(np.float32)
    ind = np.zeros((120, 4, 16), np.float32)
    for t in range(4):
        for bb in range(4):
            ind[bb * 30:(bb + 1) * 30, t, 4 * t + bb] = 1.0
    return dict(dmask=dm, zeta=zeta, xitab=xi, ropec=np.ascontiguousarray(ropec),
                ropes=np.ascontiguousarray(ropes), ident=ident, tri=tri,
                ind=ind.reshape(120, 64)), decay, gam


DEBUG = False
DM_BCAST = True
import os
STRICT = os.environ.get('KSTRICT', '1') == '1'
COARSE_POPK = os.environ.get('KCOARSE_POPK', '0') == '1'
STRICT_SE = os.environ.get('KSTRICT_SE', '0') == '1'


VR0_DWW = 11


def build_nc():
    consts, decay, gam = _consts()
    nc = bass.Bass("TRN2", target_bir_lowering=False)

    def din(name, shape):
        return nc.dram_tensor(name, list(shape), F32, kind="ExternalInput").ap()

    def dout(name, shape):
        return nc.dram_tensor(name, list(shape), F32, kind="ExternalOutput").ap()

    xp = din("xp", [SEQ, D]); xs = din("xs", [NS, D])
    sret = din("sret", [NS, 4, 128, 256]); sconv = din("sconv", [NS * 30, D])
    win = din("win", [D, 5120]); wout = din("wout", [2048, D])
    gng = din("gng", [1, D]); slg = din("slg", [1, D]); slb = din("slb", [1, D])
    sguw = din("sguw", [4, 128, 128]); sgub_d = din("sgub", [1, 512])
    cwin = din("cwin", [D, 2048]); cwout = din("cwout", [D, D])
    w1 = din("w1", [2, D, 4096]); w2 = din("w2", [2, 4096, D])
    vecs_d = din("vecs", [NVEC, D])
    c_dm = din("dmask", [128, 512]); c_zeta = din("zeta", [128, 4]); c_xi = din("xitab", [128, 512])
    c_rc = din("ropec", [128, SEQ + NS]); c_rs = din("ropes", [128, SEQ + NS])
    c_id = din("ident", [128, 128]); c_tri = din("tri", [128, 128]); c_ind = din("ind", [120, 64])

    yp = dout("yp", [SEQ, D]); ys = dout("ys", [NS, D])
    rsp = dout("rsp", [4, 128, 256]); rss = dout("rss", [NS, 4, 128, 256])
    sgur = dout("sgur", [NS, D]); csp = dout("csp", [30, D]); css = dout("css", [NS * 30, D])

    dbg = dout("dbg", [8 * KD, 128, NT]) if DEBUG else None
    S = Sched()
    es = ExitStack()
    with es:
        def sb(name, shape, dt=F32):
            return es.enter_context(nc.sbuf_tensor("sb_" + name, list(shape), dt))

        def ps(name, shape, dt=F32):
            return es.enter_context(nc.psum_tensor("ps_" + name, list(shape), dt))

        xT = sb("xT", [128, KD, NT]); hT = sb("hT", [128, KD, NT], BF16)
        identf = sb("identf", [128, 128]); identb = sb("identb", [128, 128], BF16)
        onesb = sb("onesb", [128, 128], BF16); onesf = sb("onesf", [1, 128])
        tri = sb("tri", [128, 128]); ind = sb("ind", [120, 64])
        vecT = sb("vecT", [128, KD, NVEC]); zeta = sb("zeta", [128, 4])
        WsT = sb("WsT", [128, 4, 128], BF16); sgub = sb("sgub", [1, 512])
        bias_s = sb("bias_s", [1, 64]); w00 = sb("w00", [16, 4]); Wdiag = sb("Wdiag", [16, 64], BF16)
        S_f = sb("S_f", [128, 4, 256]); S_b = sb("S_b", [128, 4, 256], BF16)
        hist = sb("hist", [128, KD, 30], BF16)
        zer16 = sb("zer16", [1, 16])
        neghalf = sb("neghalf", [128, 4])
        biasb = sb("biasb", [128, 4, 128])
        hconv = sb("hconv", [128, KD * NS])
        WB = [sb(f"WB{i}", [128, 8192], BF16) for i in range(4)]
        WS = [sb(f"WS{i}", [128, 2048], BF16) for i in range(2)]
        PB = [ps(f"PB{i}", [128, 512]) for i in range(3)]
        PO = ps("PO", [128, 1024])
        PK = ps("PK", [128, 1024])
        PT = ps("PT", [128, 1024], BF16)

        sem_names = ['pe', 'act', 'dve', 'pool', 'sp']
        sems = {n_: es.enter_context(nc.semaphore("s_" + n_)) for n_ in sem_names}

        st = dict(pb=0, wb=0, ws=0, pt=0, xin=0, so=0)

        def nextpb():
            if st.get('wide'):
                i = st['pb'] % 7
                st['pb'] += 1
                if i < 3:
                    return PB[i], f"PB{i}"
                if i < 5:
                    return PO[:, (i - 3) * 512:(i - 2) * 512], ("PO", i - 3)
                return PK[:, (i - 5) * 512:(i - 4) * 512], ("PK", i - 5)
            i = st['pb'] % 3
            st['pb'] += 1
            return PB[i], f"PB{i}"

        def nextpt():
            i = st['pt'] % 2
            st['pt'] += 1
            return PT[:, i * 512:(i + 1) * 512], f"PT{i}"

        def nextwb():
            i = st['wb'] % 4
            st['wb'] += 1
            return WB[i], f"WB{i}", f"wb{i}"

        def nextws():
            i = st['ws'] % 2
            st['ws'] += 1
            return WS[i], f"WS{i}", f"ws{i}"

        def mm(out, lhsT, rhs, start, stop, r, w):
            S.add('pe', lambda e: e.matmul(out, lhsT, rhs, start=start, stop=stop), r, w)

        def tp(out, in_, ident, r, w):
            S.add('pe', lambda e: e.transpose(out, in_, ident), r, w)

        def act(out, in_, func, r, w, bias=None, scale=None):
            kw = {}
            if bias is not None:
                kw['bias'] = bias
            if scale is not None:
                kw['scale'] = scale
            S.add('act', lambda e: e.activation(out, in_, func, **kw), r, w)

        def tt(eng, out, in0, in1, op, r, w):
            S.add(eng, lambda e: e.tensor_tensor(out, in0, in1, op), r, w)

        def ts(eng, out, in0, s1, s2, op0, op1, r, w):
            if s2 is None:
                S.add(eng, lambda e: e.tensor_scalar(out, in0, s1, None, op0), r, w)
            else:
                S.add(eng, lambda e: e.tensor_scalar(out, in0, s1, s2, op0, op1), r, w)

        def stt(out, in0, sc, in1, op0, op1, r, w):
            S.add('dve', lambda e: e.scalar_tensor_tensor(out, in0, sc, in1, op0, op1), r, w)

        def cp(eng, out, in_, r, w):
            if eng == 'act':
                S.add('act', lambda e: e.activation(out, in_, AF.Copy), r, w)
            else:
                S.add(eng, lambda e: e.tensor_copy(out, in_), r, w)

        def recip(out, in_, r, w):
            S.add('dve', lambda e: e.reciprocal(out, in_), r, w)

        def bnstats(out, in_, r, w):
            S.add('dve', lambda e: e.bn_stats(out, in_), r, w)

        def bnaggr(out, in_, r, w):
            S.add('dve', lambda e: e.bn_aggr(out, in_), r, w)

        def memset(ap, val, w):
            S.add('dve', lambda e: e.memset(ap, val), [], w)

        def rstd_pool(out, tmp, var_ap, eps, r, tk, w):
            np_ = out.shape[0]
            nco = out.shape[1]
            S.add('pool', lambda e: e.tensor_scalar(tmp, var_ap, float(eps), 1.0, ALU.add, ALU.mult), r, [tk])
            S.add('pool', lambda e: e.tensor_tensor(out, tmp, neghalf[0:np_, 0:nco], ALU.pow), [tk, "neghalf"], w)

        def dma(q, out, in_, r, w, chan):
            S.add(q, lambda e: e.dma_start(out=out, in_=in_), r, w, chan=chan)

        def xk(k, tgi):
            return ("xT", k, tgi)

        def dumpt(tag, ap, keys, shape, dt=F32):
            if DEBUG:
                t = nc.dram_tensor("dbg_" + tag, list(shape), dt, kind="ExternalOutput").ap()
                dma('sp', t, ap, keys, [], 'o_dbg')

        def dump(idx):
            if DEBUG:
                for k in range(KD):
                    dma('sp', dbg[idx * KD + k], xT[:, k, :], [xk(k, t) for t in range(3)], [], 'o_dbg')

        def hk(tgi):
            return ("hT", tgi)

        dma('sp', identf[:, :], c_id, [], ["identf"], 'const')
        dma('sp', tri[:, :], c_tri, [], ["tri"], 'const')
        dma('sp', ind[:, :], c_ind, [], ["ind"], 'const')
        dma('sp', zeta[:, :], c_zeta, [], ["zeta"], 'const')
        dma('sp', sgub[:, :], sgub_d, [], ["sgub"], 'const')
        dma('sp', biasb[:, :, :].rearrange("p g i -> p (g i)"), sgub_d[0:1, :].partition_broadcast(128), [], ["biasb"], 'const')
        for g in range(4):
            dma('sp', w00[:, g:g + 1], sguw[g, 0:1, 0:1].partition_broadcast(16), [], ["w00"], 'const')
        S.add('dve', lambda e: e.tensor_copy(identb[:, :], identf[:, :]), ["identf"], ["identb"])
        S.add('dve', lambda e: e.memset(onesb[:, :], 1.0), [], ["onesb"])
        S.add('dve', lambda e: e.memset(onesf[:, :], 1.0), [], ["onesf"])
        S.add('dve', lambda e: e.memset(zer16[:, :], 0.0), [], ["zer16"])
        S.add('dve', lambda e: e.memset(neghalf[:, :], -0.5), [], ["neghalf"])
        S.add('dve', lambda e: e.memset(S_f[:, :, :], 0.0), [], [("S_f", h) for h in range(4)])
        S.add('dve', lambda e: e.memset(S_b[:, :, :], 0.0), [], [("S_b", h) for h in range(4)])
        S.add('dve', lambda e: e.memset(hist[:, :, :], 0.0), [], ["hist"])
        for g in range(4):
            ts('dve', bias_s[0:1, g * 16:(g + 1) * 16], zer16[0:1, :], sgub[0:1, g * 128:g * 128 + 1], None,
               ALU.add, None, ["zer16", "sgub"], ["bias_s"])
            ts('dve', Wdiag[:, g * 16:(g + 1) * 16], identf[0:16, 0:16], w00[:, g:g + 1], None,
               ALU.mult, None, ["identf", "w00"], ["Wdiag"])
        with ExitStack() as ph:
            vecs = ph.enter_context(nc.sbuf_tensor("sb_vecs", [NVEC, D], F32))
            swt = ph.enter_context(nc.sbuf_tensor("sb_swt", [128, 512], F32))
            dma('sp', vecs[:, :], vecs_d, [], ["vecs"], 'const')
            dma('sp', swt[:, :].rearrange("p (g j) -> p g j", g=4), sguw.rearrange("g i j -> i g j"),
                [], ["swt"], 'const')
            pb, pk = nextpb()
            for k in range(KD):
                tp(pb[:, k * NVEC:(k + 1) * NVEC], vecs[:, k * 128:(k + 1) * 128], identf[0:NVEC, 0:NVEC],
                   ["vecs", "identf"], [pk])
            cp('act', vecT[:, :, :], pb[:, 0:KD * NVEC].rearrange("p (k r) -> p k r", k=KD), [pk], ["vecT"])
            pb, pk = nextpb()
            for g in range(4):
                tp(pb[:, g * 128:(g + 1) * 128], swt[:, g * 128:(g + 1) * 128], identf[:, :], ["swt", "identf"], [pk])
            for g in range(4):
                tt('dve', WsT[:, g, :], pb[:, g * 128:(g + 1) * 128], tri[:, :], ALU.mult, [pk, "tri"], ["WsT"])
            dma('sp', css.rearrange("(b j) d -> b j d", j=30)[:, 0:29, :],
                sconv.rearrange("(b j) d -> b j d", j=30)[:, 1:30, :], [], [], 'o_css')
            hs_ = [ph.enter_context(nc.sbuf_tensor(f"sb_hs{i}", [120, D], F32)) for i in range(2)]
            wrep = ph.enter_context(nc.sbuf_tensor("sb_wrep", [120, D], F32))
            for r4 in range(4):
                dma('sp', wrep[r4 * 30:(r4 + 1) * 30, :], vecs_d[VR0_DWW:VR0_DWW + 30, :], [], ["wrep"], 'const')
            for t4 in range(4):
                hi = t4 % 2
                dma('sp', hs_[hi][:, :], sconv[t4 * 120:(t4 + 1) * 120, :], [], [f"hs{hi}"], f't_hs{hi}')
                tt('dve', hs_[hi][:, :], hs_[hi][:, :], wrep[:, :], ALU.mult, [f"hs{hi}", "wrep"], [f"hs{hi}"])
                for cc in range(KD):
                    mm(PO[:, cc * NS:(cc + 1) * NS], hs_[hi][:, cc * 128:(cc + 1) * 128],
                       ind[:, t4 * 16:(t4 + 1) * 16], t4 == 0 and cc == 0, t4 == 3 and cc == KD - 1,
                       [f"hs{hi}", "ind"], [("PO", 0)])
            cp('act', hconv[:, :], PO[:, 0:KD * NS], [("PO", 0)], ["hconv"])
            S.barrier()

        pre = {}
        VR = dict(nm0=0, nm1=1, nf0=2, nf1=3, fin=4, cba=5, cbg=6, dwb=7, clg=8, clb=9, cbo=10, dww=11)

        for p in range(NPASS):
            last = (p == NPASS - 1)
            tgs = [(0, 512), (512, 512)] + ([(P, NS)] if last else [])

            def load_x_pre(pp, ph_):
                xin = [ph_.enter_context(nc.sbuf_tensor(f"xin{i}_{pp}", [128, D], F32)) for i in range(2)]
                for c in range(2):
                    dma('sp', xin[c][:, :], xp[pp * P + c * 128:pp * P + (c + 1) * 128, :], [], [f"xin{c}"], f'xin{c}')
                return xin

            def load_x(pp, ph_, xin=None):
                lastp = (pp == NPASS - 1)
                pre = xin is not None
                if not pre:
                    xin = [ph_.enter_context(nc.sbuf_tensor(f"xin{i}_{pp}", [128, D], F32)) for i in range(2)]
                for c in range(P // 128):
                    i = c % 2
                    if not (pre and c < 2):
                        dma('sp', xin[i][:, :], xp[pp * P + c * 128:pp * P + (c + 1) * 128, :], [], [f"xin{i}"], f'xin{i}')
                    for half in range(2):
                        pb, pk = nextpb()
                        for kk in range(4):
                            k = half * 4 + kk
                            tp(pb[:, kk * 128:(kk + 1) * 128], xin[i][:, k * 128:(k + 1) * 128], identf[:, :],
                               [f"xin{i}", "identf"], [pk])
                        cp('act' if half == 0 else 'dve', xT[:, half * 4:half * 4 + 4, c * 128:(c + 1) * 128],
                           pb[:, :].rearrange("p (k t) -> p k t", k=4), [pk],
                           [xk(k, c // 4) for k in range(half * 4, half * 4 + 4)])
                if lastp:
                    dma('sp', xin[0][0:NS, :], xs, [], ["xin0"], 'xin0')
                    pb, pk = nextpb()
                    for k in range(KD):
                        tp(pb[:, k * NS:(k + 1) * NS], xin[0][0:NS, k * 128:(k + 1) * 128], identf[0:NS, 0:NS],
                           ["xin0", "identf"], [pk])
                    cp('act', xT[:, :, P:P + NS], pb[:, 0:KD * NS].rearrange("p (k t) -> p k t", k=KD), [pk],
                       [xk(k, 2) for k in range(KD)])

            if p == 0:
                with ExitStack() as ph:
                    load_x(0, ph)
                    S.barrier()

            def rmsnorm(vrow, ph, tag, ext=None):
                if ext is None:
                    sq = ph.enter_context(nc.sbuf_tensor(f"sq{tag}", [128, KD, 512], BF16))
                    sd = [ph.enter_context(nc.sbuf_tensor(f"sd{tag}{i}", [128, 512], F32)) for i in range(2)]
                    rs = [ph.enter_context(nc.sbuf_tensor(f"rs{tag}{i}", [128, 512], F32)) for i in range(2)]
                    sqk, sdk, rsk = ["sq"], ["sd0", "sd1"], ["rs0", "rs1"]
                else:
                    sq, sd1, rs1, sqk, sdk1, rsk1 = ext
                    sd = [sd1, sd1]; rs = [rs1, rs1]; sdk = [sdk1, sdk1]; rsk = [rsk1, rsk1]
                for tgi, (c0, n) in enumerate(tgs):
                    i = tgi % 2
                    act(sq[:, :, 0:n], xT[:, :, c0:c0 + n], AF.Square, [xk(k, tgi) for k in range(KD)], sqk)
                    pb, pk = nextpb()
                    for k in range(KD):
                        mm(pb[:, 0:n], onesb[:, :], sq[:, k, 0:n], k == 0, k == KD - 1, [sqk[0], "onesb"], [pk])
                    act(sd[i][:, 0:n], pb[:, 0:n], AF.Sqrt, [pk], [sdk[i]], bias=float(RMS_EPS), scale=1.0 / D)
                    recip(rs[i][:, 0:n], sd[i][:, 0:n], [sdk[i]], [rsk[i]])
                    for k in range(KD):
                        stt(hT[:, k, c0:c0 + n], xT[:, k, c0:c0 + n], vecT[:, k, vrow:vrow + 1], rs[i][:, 0:n],
                            ALU.mult, ALU.mult, [xk(k, tgi), "vecT", rsk[i]], [hk(tgi)])

            def wload(dst, src, wkey, chan):
                dma('pool', dst, src, [], [wkey], chan)

            def out_proj(wo, wokey, rT, rkeys, tgi, c0, n):
                for d in range(KD):
                    pb, pk = nextpb()
                    for kk in range(2):
                        mm(pb[:, 0:n], wo[:, kk * 1024 + d * 128:kk * 1024 + (d + 1) * 128], rT[:, kk, 0:n],
                           kk == 0, kk == 1, [wokey] + rkeys, [pk])
                    tt('dve', xT[:, d, c0:c0 + n], xT[:, d, c0:c0 + n], pb[:, 0:n], ALU.add,
                       [xk(d, tgi), pk], [xk(d, tgi)])

            def ffn_load(l, j):
                wa, wak, wac = nextwb()
                wb_, wbk, wbc = nextwb()
                wa3 = wa[:, :].rearrange("p (k c) -> p k c", k=KD)
                wb3 = wb_[:, :].rearrange("p (k c) -> p k c", k=KD)
                for hh in range(2):
                    wload(wa3[:, hh * 4:hh * 4 + 4, :],
                          w1[l].rearrange("(k p) c -> p k c", p=128)[:, hh * 4:hh * 4 + 4, j * 1024:(j + 1) * 1024],
                          wak, wac)
                for hh in range(2):
                    wload(wb3[:, hh * 4:hh * 4 + 4, :],
                          w2[l][j * 1024:(j + 1) * 1024, :].rearrange("(k p) c -> p k c", p=128)[:, hh * 4:hh * 4 + 4, :],
                          wbk, wbc)
                return wa3, wak, wb3, wbk

            def conv_load_ab():
                wa, wak, wac = nextwb()
                wb_, wbk, wbc = nextwb()
                wa3 = wa[:, :].rearrange("p (k c) -> p k c", k=KD)
                wb3 = wb_[:, :].rearrange("p (k c) -> p k c", k=KD)
                cw3_ = cwin.rearrange("(k p) c -> p k c", p=128)
                for hh in range(2):
                    wload(wa3[:, hh * 4:hh * 4 + 4, :], cw3_[:, hh * 4:hh * 4 + 4, 0:1024], wak, wac)
                for hh in range(2):
                    wload(wb3[:, hh * 4:hh * 4 + 4, :], cw3_[:, hh * 4:hh * 4 + 4, 1024:2048], wbk, wbc)
                pre['conv'] = (wa3, wak, wb3, wbk)

            def ffn(l, ph, tag, pre0=None, hook3=None):
                fT = [ph.enter_context(nc.sbuf_tensor(f"fT{tag}{i}", [128, KD, 512 if i < 2 else NS], BF16))
                      for i in range(len(tgs))]
                rl = [ph.enter_context(nc.sbuf_tensor(f"rl{tag}{i}", [128, 512], F32)) for i in range(2)]
                for j in range(4):
                    wa3, wak, wb3, wbk = pre0 if (j == 0 and pre0 is not None) else ffn_load(l, j)
                    if j == 3 and hook3 is not None:
                        hook3()
                    for tgi, (c0, n) in enumerate(tgs):
                        for fk in range(KD):
                            pb, pk = nextpb()
                            for k in range(KD):
                                mm(pb[:, 0:n], wa3[:, k, fk * 128:(fk + 1) * 128], hT[:, k, c0:c0 + n], k == 0, k == KD - 1,
                                   [wak, hk(tgi)], [pk])
                            ri = fk % 2
                            act(rl[ri][:, 0:n], pb[:, 0:n], AF.Relu, [pk], [f"rl{ri}"])
                            tt('dve', fT[tgi][:, fk, 0:n], rl[ri][:, 0:n], rl[ri][:, 0:n], ALU.mult, [f"rl{ri}"], [(f"fT{tgi}", fk)])
                    for tgi, (c0, n) in enumerate(tgs):
                        for d in range(KD):
                            pb, pk = nextpb()
                            for fk in range(KD):
                                mm(pb[:, 0:n], wb3[:, fk, d * 128:(d + 1) * 128], fT[tgi][:, fk, 0:n], fk == 0, fk == KD - 1,
                                   [wbk] + [(f"fT{tgi}", fk)], [pk])
                            tt('dve', xT[:, d, c0:c0 + n], xT[:, d, c0:c0 + n], pb[:, 0:n], ALU.add,
                               [xk(d, tgi), pk], [xk(d, tgi)])

            with ExitStack() as ph:
                rmsnorm(VR['nm0'], ph, f"a{p}")
                S.barrier()

            with ExitStack() as ph:
                def T(name, shape, dt=F32):
                    return ph.enter_context(nc.sbuf_tensor(f"sb_{name}_{p}", list(shape), dt))
                cosT = T("cosT", [128, NT]); sinT = T("sinT", [128, NT])
                xitab = T("xitab", [128, 4, 128]); dmask = T("dmask", [128, 4, 128])
                gngb = [T(f"gngb{i}", [128, 256]) for i in range(2)]
                t1 = T("t1", [128, 512]); t2 = T("t2", [128, 512])
                qT = [T(f"qT{i}", [128, 512], BF16) for i in range(2)]
                kT = [T(f"kT{i}", [128, 512], BF16) for i in range(2)]
                qx = [T(f"qx{i}", [128, 512], BF16) for i in range(2)]
                v4 = [T(f"v4{i}", [128, 4, 256], BF16) for i in range(2)]
                gs4 = [T(f"gs4{i}", [128, 4, 256], BF16) for i in range(3)]
                inm4 = [T("inm40", [128, 4, 128], BF16)] * 2
                kz4 = [T("kz40", [128, 4, 128], BF16)] * 2
                Sb4 = [T("Sb40", [128, 4, 256], BF16)] * 2
                osb4 = [T(f"osb4{i}", [128, 4, 256]) for i in range(2)]
                rr4 = [T(f"rr4{i}", [128, 4, 256], BF16) for i in range(2)]
                rT = [T(f"rT{i}", [128, 2, 512], BF16) for i in range(2)]
                bst4 = [T(f"bst4{i}", [128, 4, 6]) for i in range(2)]
                mv4 = [T(f"mv4{i}", [128, 4, 2]) for i in range(2)]
                sdv4 = [T(f"sdv4{i}", [128, 4]) for i in range(2)]
                rsv4 = [T(f"rsv4{i}", [128, 4]) for i in range(2)]
                if last:
                    qsf = T("qsf", [128, NS]); ksf = T("ksf", [128, NS]); qm = T("qm", [128, 272])
                    kTM = T("kTM", [NS, 128]); km4 = T("km4", [NS, 4, 128])
                    vsf = T("vsf", [NS, 256])
                    so = [T(f"so{i}", [128, 256]) for i in range(4)]; sn = [T(f"sn{i}", [128, 256]) for i in range(4)]
                    gs_s = T("gs_s", [NS, 256]); osb_s = T("osb_s", [NS, 256]); rr_s = T("rr_s", [NS, 256], BF16)
                    rT_s = T("rT_s", [128, 2, NS], BF16)
                    bst_s = T("bst_s", [NS, 6]); mv_s = T("mv_s", [NS, 2]); sdv_s = T("sdv_s", [NS, 1]); rsv_s = T("rsv_s", [NS, 1])

                dma('sp', cosT[:, 0:P], c_rc[:, p * P:(p + 1) * P], [], ["cosT"], 't_cos')
                dma('sp', sinT[:, 0:P], c_rs[:, p * P:(p + 1) * P], [], ["sinT"], 't_sin')
                if last:
                    dma('sp', cosT[:, P:NT], c_rc[:, SEQ:SEQ + NS], [], ["cosT"], 't_cos')
                    dma('sp', sinT[:, P:NT], c_rs[:, SEQ:SEQ + NS], [], ["sinT"], 't_sin')
                dma('sp', xitab[:, :, :].rearrange("p h n -> p (h n)"), c_xi, [], ["xitab"], 't_xi')
                dma('sp', dmask[:, :, :].rearrange("p h n -> p (h n)"), c_dm, [], ["dmask"], 't_dm')
                if last:
                    memset(qm[:, :], 0.0, ["qm"])

                win3 = win.rearrange("(k p) c -> p k c", p=128)
                heads = {}
                hwo = {}

                def load_wh(h):
                    wh, whk, whc = nextwb()
                    wh3 = wh[:, :].rearrange("p (k c) -> p k c", k=KD)
                    wload(wh3[:, :, 0:128], win3[:, :, h * 128:(h + 1) * 128], whk, whc)
                    wload(wh3[:, :, 128:256], win3[:, :, 512 + h * 128:512 + (h + 1) * 128], whk, whc)
                    wload(wh3[:, :, 256:512], win3[:, :, 1024 + h * 256:1024 + (h + 1) * 256], whk, whc)
                    wload(wh3[:, :, 512:768], win3[:, :, 2048 + h * 256:2048 + (h + 1) * 256], whk, whc)
                    cp('act', wh3[:, :, 768:832], wh3[:, :, 64:128], [whk], [whk + "sq"])
                    cp('act', wh3[:, :, 832:896], wh3[:, :, 0:64], [whk], [whk + "sq"])
                    cp('dve', wh3[:, :, 896:960], wh3[:, :, 192:256], [whk], [whk + "sk"])
                    cp('dve', wh3[:, :, 960:1024], wh3[:, :, 128:192], [whk], [whk + "sk"])
                    heads[h] = (wh3, [whk, whk + "sq", whk + "sk"])

                def load_wo(h):
                    wo, wok, woc = nextws()
                    wload(wo[:, :].rearrange("p (kk d) -> p kk d", kk=2),
                          wout[h * 256:(h + 1) * 256, :].rearrange("(kk p) d -> p kk d", p=128), wok, woc)
                    hwo[h] = (wo, wok)
                    dma('sp', gngb[h % 2][:, :], gng[0:1, h * 256:(h + 1) * 256].partition_broadcast(128), [],
                        [f"gngb{h % 2}"], f't_gng{h % 2}')

                def proj_qk(h, tgi, c0, n, which):
                    wh3, wr = heads[h]
                    zp = {}
                    for nm_, cs in ((("q", 0), ("qs", 768)) if which == "q" else (("k", 128), ("ks", 896))):
                        pb, pk = nextpb()
                        for k in range(KD):
                            mm(pb[:, 0:n], wh3[:, k, cs:cs + 128], hT[:, k, c0:c0 + n], k == 0, k == KD - 1,
                               wr + [hk(tgi)], [pk])
                        zp[nm_] = (pb, pk)
                    return zp

                def rot(zp, a, b_, dst, dkey, c0, n):
                    tt('dve', t1[:, 0:n], zp[a][0][:, 0:n], cosT[:, c0:c0 + n], ALU.mult, [zp[a][1], "cosT"], ["t1"])
                    tt('dve', t2[:, 0:n], zp[b_][0][:, 0:n], sinT[:, c0:c0 + n], ALU.mult, [zp[b_][1], "sinT"], ["t2"])
                    tt('dve', dst, t1[:, 0:n], t2[:, 0:n], ALU.add, ["t1", "t2"], [dkey])

                def stepsA1(h, tgi, ui):
                    u = ui % 2
                    g3 = ui % 3
                    wh3, wr = heads[h]
                    c0, n = tgs[tgi]

                    def s0():
                        zp = proj_qk(h, tgi, c0, n, "q")
                        rot(zp, "q", "qs", qT[u][:, :], f"qT{u}", c0, n)

                    def s1():
                        zp = proj_qk(h, tgi, c0, n, "k")
                        rot(zp, "k", "ks", kT[u][:, :], f"kT{u}", c0, n)
                        tt('dve', qx[u][:, :].rearrange("p (c n) -> p c n", c=4), qT[u][:, :].rearrange("p (c n) -> p c n", c=4),
                           xitab[:, h:h + 1, :].to_broadcast([128, 4, 128]), ALU.mult, [f"qT{u}", "xitab"], [f"qx{u}"])

                    def mk(cc):
                        def s():
                            cs_ = c0 + cc * 128
                            pv, pvk = nextpb()
                            for k in range(KD):
                                mm(pv[:, 0:256], hT[:, k, cs_:cs_ + 128], wh3[:, k, 256:512], k == 0, k == KD - 1,
                                   wr + [hk(tgi)], [pvk])
                            cp('act', v4[u][:, cc, :], pv[:, 0:256], [pvk], [f"v4{u}"])
                            pg, pgk = nextpb()
                            for k in range(KD):
                                mm(pg[:, 0:256], hT[:, k, cs_:cs_ + 128], wh3[:, k, 512:768], k == 0, k == KD - 1,
                                   wr + [hk(tgi)], [pgk])
                            act(gs4[g3][:, cc, :], pg[:, 0:256], AF.Silu, [pgk], [f"gs4{g3}"])
                        return s
                    return [s0, s1] + [mk(cc) for cc in range(4)]

                def stepsA2(h, tgi, ui):
                    u = ui % 2

                    def t0():
                        pi_, pik = nextpb()
                        for cc in range(4):
                            lc = cc * 128
                            mm(pi_[:, lc:lc + 128], kT[u][:, lc:lc + 128], qT[u][:, lc:lc + 128], True, True,
                               [f"kT{u}", f"qT{u}"], [pik])
                        for cc in range(4):
                            lc = cc * 128
                            tp(PT[:, lc:lc + 128], kT[u][:, lc:lc + 128], identb[:, :], [f"kT{u}", "identb"], ["PT0"])
                        if DM_BCAST:
                            tt('dve', inm4[u][:, :, :], pi_[:, 0:512].rearrange("p (c n) -> p c n", c=4),
                               dmask[:, h:h + 1, :].to_broadcast([128, 4, 128]), ALU.mult, [pik, "dmask"], ["inm4"])
                        else:
                            for cc in range(4):
                                tt('dve', inm4[u][:, cc, :], pi_[:, cc * 128:(cc + 1) * 128], dmask[:, h, :], ALU.mult,
                                   [pik, "dmask"], ["inm4"])
                        ts('dve', kz4[u][:, :, :].rearrange("p c n -> p (c n)"), PT[:, 0:512], zeta[:, h:h + 1], None,
                           ALU.mult, None, ["PT0", "zeta"], ["kz4"])

                    def t1_():
                        for cc in range(4):
                            mm(PK[:, cc * 256:(cc + 1) * 256], kz4[u][:, cc, :], v4[u][:, cc, :], True, True,
                               ["kz4", f"v4{u}"], [("PK", cc // 2)])

                    def t2_():
                        for cc in range(4):
                            stt(S_f[:, h, :], S_f[:, h, :], decay[h], PK[:, cc * 256:(cc + 1) * 256], ALU.mult, ALU.add,
                                [("S_f", h), ("PK", cc // 2)], [("S_f", h)])
                            if cc < 3:
                                cp('act', Sb4[u][:, cc + 1, :], S_f[:, h, :], [("S_f", h)], [("Sb4", cc + 1)])

                    def t3():
                        for cc in range(4):
                            lc = cc * 128
                            pok = ("PO", cc // 2)
                            mm(PO[:, cc * 256:(cc + 1) * 256], inm4[u][:, cc, :], v4[u][:, cc, :], True, False,
                               ["inm4", f"v4{u}"], [pok])
                            if cc == 0:
                                mm(PO[:, 0:256], qx[u][:, lc:lc + 128], S_b[:, h, :], False, True, [f"qx{u}", ("S_b", h)], [pok])
                            else:
                                mm(PO[:, cc * 256:(cc + 1) * 256], qx[u][:, lc:lc + 128], Sb4[u][:, cc, :], False, True,
                                   [f"qx{u}", ("Sb4", cc)], [pok])
                            cp('act', osb4[u][:, cc, :], PO[:, cc * 256:(cc + 1) * 256], [pok], [f"osb4{u}"])
                        cp('act', S_b[:, h, :], S_f[:, h, :], [("S_f", h)], [("S_b", h)])
                    return [t0, t1_, t2_, t3]

                def stepsB(h, tgi, ui):
                    u = ui % 2
                    g3 = ui % 3
                    c0, n = tgs[tgi]
                    gb = gngb[h % 2]
                    gbk = f"gngb{h % 2}"

                    def b0():
                        for cc in range(4):
                            bnstats(bst4[u][:, cc, :], osb4[u][:, cc, :], [f"osb4{u}"], [f"bst4{u}"])
                        for cc in range(4):
                            bnaggr(mv4[u][:, cc, :], bst4[u][:, cc, :], [f"bst4{u}"], [f"mv4{u}"])
                        rstd_pool(rsv4[u][:, :], sdv4[u][:, :], mv4[u][:, :, 1], LN_EPS, [f"mv4{u}"], f"sdv4{u}", [f"rsv4{u}"])
                        for cc in range(4):
                            ts('dve', osb4[u][:, cc, :], osb4[u][:, cc, :], mv4[u][:, cc, 0:1], rsv4[u][:, cc:cc + 1],
                               ALU.subtract, ALU.mult, [f"osb4{u}", f"mv4{u}", f"rsv4{u}"], [f"osb4{u}"])
                        tt('dve', osb4[u][:, :, :], osb4[u][:, :, :],
                           gb[:, :].rearrange("p (o n) -> p o n", o=1).to_broadcast([128, 4, 256]), ALU.mult,
                           [f"osb4{u}", gbk], [f"osb4{u}"])
                        tt('dve', rr4[u][:, :, :], osb4[u][:, :, :], gs4[g3][:, :, :], ALU.mult,
                           [f"osb4{u}", f"gs4{g3}"], [f"rr4{u}"])

                    def b1():
                        for cc in range(4):
                            for e2 in range(2):
                                tp(PT[:, e2 * 512 + cc * 128:e2 * 512 + (cc + 1) * 128], rr4[u][:, cc, e2 * 128:(e2 + 1) * 128],
                                   identb[:, :], [f"rr4{u}", "identb"], ["PT0"])
                        cp('act', rT[u][:, :, :].rearrange("p e t -> p (e t)"), PT[:, 0:1024], ["PT0"], [f"rT{u}"])

                    def b2():
                        wo, wok = hwo[h]
                        out_proj(wo, wok, rT[u], [f"rT{u}"], tgi, c0, n)
                    return [b0, b1, b2]

                def gn_gate_s(o_ps, okey, h):
                    gb = gngb[h % 2]
                    gbk = f"gngb{h % 2}"
                    cp('act', osb_s[:, :], o_ps, [okey], ["osb_s"])
                    bnstats(bst_s[:, :], osb_s[:, :], ["osb_s"], ["bst_s"])
                    bnaggr(mv_s[:, :], bst_s[:, :], ["bst_s"], ["mv_s"])
                    rstd_pool(rsv_s[:, :], sdv_s[:, :], mv_s[:, 1:2], LN_EPS, ["mv_s"], "sdv_s", ["rsv_s"])
                    ts('dve', osb_s[:, :], osb_s[:, :], mv_s[:, 0:1], rsv_s[:, 0:1], ALU.subtract, ALU.mult,
                       ["osb_s", "mv_s", "rsv_s"], ["osb_s"])
                    tt('dve', osb_s[:, :], osb_s[:, :], gb[0:NS, :], ALU.mult, ["osb_s", gbk], ["osb_s"])
                    tt('dve', rr_s[:, :], osb_s[:, :], gs_s[:, :], ALU.mult, ["osb_s", "gs_s"], ["rr_s"])

                def sample_head(h):
                    wh3, wr = heads[h]
                    wo, wok = hwo[h]
                    tgi = 2
                    c0, n = tgs[tgi]
                    zp = proj_qk(h, tgi, c0, n, "q")
                    rot(zp, "q", "qs", qsf[:, :], "qsf", c0, n)
                    zp = proj_qk(h, tgi, c0, n, "k")
                    rot(zp, "k", "ks", ksf[:, :], "ksf", c0, n)
                    cp('dve', qm[:, :].rearrange("p (a c) -> p a c", c=17)[:, :, 0], qsf[:, :], ["qsf"], ["qm"])
                    pb, pk = nextpb()
                    tp(pb[0:NS, 0:128], ksf[:, :], identf[:, :], ["ksf", "identf"], [pk])
                    ts('dve', kTM[:, :], pb[0:NS, 0:128], float(128 ** -0.5), None, ALU.mult, None, [pk], ["kTM"])
                    pv, pvk = nextpb()
                    for k in range(KD):
                        mm(pv[0:NS, 0:512], hT[:, k, c0:c0 + n], wh3[:, k, 256:768], k == 0, k == KD - 1,
                           wr + [hk(tgi)], [pvk])
                    cp('act', vsf[:, :], pv[0:NS, 0:256], [pvk], ["vsf"])
                    act(gs_s[:, :], pv[0:NS, 256:512], AF.Silu, [pvk], ["gs_s"])
                    pok = ("PO", 0)
                    for b in range(4):
                        dma('sp', so[b][:, :], sret[b, h], [], [f"so{b}"], f'so{b}')
                    for b0 in range(0, NS, 4):
                        for bb in range(4):
                            b = b0 + bb
                            ts('dve', km4[:, bb, :], kTM[:, :], identf[0:NS, b:b + 1], None, ALU.mult, None,
                               ["kTM", "identf"], ["km4"])
                        for bb in range(4):
                            mm(PK[:, bb * 256:(bb + 1) * 256], km4[:, bb, :], vsf[:, :], True, True, ["km4", "vsf"],
                               [("PK", bb // 2)])
                        for bb in range(4):
                            b = b0 + bb
                            stt(sn[bb][:, :], so[bb][:, :], gam[h], PK[:, bb * 256:(bb + 1) * 256], ALU.mult, ALU.add,
                                [f"so{bb}", ("PK", bb // 2)], [f"sn{bb}"])
                            dma('sp', rss[b, h], sn[bb][:, :], [f"sn{bb}"], [], f'o_sn{bb}')
                            if b + 4 < NS:
                                dma('sp', so[bb][:, :], sret[b + 4, h], [], [f"so{bb}"], f'so{bb}')
                        for bb in range(4):
                            b = b0 + bb
                            mm(PO[0:NS, 0:256], qm[:, b * 16:(b + 1) * 16], sn[bb][:, :], b == 0, b == NS - 1,
                               ["qm", f"sn{bb}"], [pok])
                    gn_gate_s(PO[0:NS, 0:256], pok, h)
                    pt_, ptk = nextpt()
                    for e2 in range(2):
                        tp(pt_[:, e2 * NS:(e2 + 1) * NS], rr_s[:, e2 * 128:(e2 + 1) * 128], identb[0:NS, 0:NS],
                           ["rr_s", "identb"], [ptk])
                    cp('act', rT_s[:, :, :], pt_[:, 0:2 * NS].rearrange("p (e t) -> p e t", e=2), [ptk], ["rT_s"])
                    out_proj(wo, wok, rT_s, ["rT_s"], tgi, c0, n)

                units = [(h, tgi) for h in range(4) for tgi in range(2)]
                NU = len(units)
                load_wh(0)
                load_wo(0)
                order = [("A2", 0), ("A1", 0), ("A2", 1), ("A1", 1), ("A2", 2), ("B", 0), ("A1", 2), ("A1", 3), ("A2", 3),
                         ("A1", 4), ("B", 1), ("A1", 5), ("B", 2)]
                for it in range(NU + 2):
                    if it % 2 == 0:
                        h_ = it // 2
                        if h_ + 1 < 4:
                            load_wh(h_ + 1)
                        if 1 <= h_ < 4:
                            load_wo(h_)
                    st_ = {}
                    if it < NU:
                        st_["A1"] = stepsA1(units[it][0], units[it][1], it)
                    if 0 <= it - 1 < NU:
                        st_["A2"] = stepsA2(units[it - 1][0], units[it - 1][1], it - 1)
                    if 0 <= it - 2 < NU:
                        st_["B"] = stepsB(units[it - 2][0], units[it - 2][1], it - 2)
                    for nm_, si in order:
                        if nm_ in st_:
                            st_[nm_][si]()
                    if 0 <= it - 2 < NU and units[it - 2][1] == 1:
                        hd = units[it - 2][0]
                        if last:
                            sample_head(hd)
                        if last:
                            dma('sp', rsp[hd], S_f[:, hd, :], [("S_f", hd)], [], 'o_Sf')
                S.barrier()

            with ExitStack() as ph:
                def T(name, shape, dt=F32):
                    return ph.enter_context(nc.sbuf_tensor(f"sb_{name}_g{p}", list(shape), dt))
                slgb = T("slgb", [128, D]); slbb = T("slbb", [128, D])
                ug = [T(f"ug{i}", [128, 2, 512]) for i in range(2)]
                sg4 = [T(f"sg4{i}", [128, 4, 256]) for i in range(2)]
                snb4 = [T(f"snb4{i}", [128, 4, 256], BF16) for i in range(2)]
                bst4 = [T(f"bst4{i}", [128, 4, 6]) for i in range(2)]
                mv4 = [T(f"mv4{i}", [128, 4, 2]) for i in range(2)]
                sdv4 = [T(f"sdv4{i}", [128, 4]) for i in range(2)]
                rsv4 = [T(f"rsv4{i}", [128, 4]) for i in range(2)]
                goT = [T(f"goT{i}", [128, 2, 512], BF16) for i in range(2)]
                mb = T("mb", [128, 512])
                dma('sp', slgb[:, :], slg[0:1, :].partition_broadcast(128), [], ["slgb"], 't_slg')
                dma('sp', slbb[:, :], slb[0:1, :].partition_broadcast(128), [], ["slbb"], 't_slb')
                win3 = win.rearrange("(k p) c -> p k c", p=128)
                groups = {}

                def load_group(g):
                    wg_, wgk, wgc = nextwb()
                    wo, wok, woc = nextws()
                    wg3 = wg_[:, 0:KD * 512].rearrange("p (k c) -> p k c", k=KD)
                    wload(wg3[:, :, 0:256], win3[:, :, 3072 + g * 256:3072 + (g + 1) * 256], wgk, wgc)
                    wload(wg3[:, :, 256:512], win3[:, :, 4096 + g * 256:4096 + (g + 1) * 256], wgk, wgc)
                    wload(wo[:, :].rearrange("p (kk d) -> p kk d", kk=2),
                          wout[1024 + g * 256:1024 + (g + 1) * 256, :].rearrange("(kk p) d -> p kk d", p=128), wok, woc)
                    groups[g] = (wg3, wgk, wo, wok)

                def sguA(g, tgi, u):
                    wg3, wgk, wo, wok = groups[g]
                    c0, n = tgs[tgi]
                    samp = (n == NS)
                    for dd in range(2):
                        pb, pk = nextpb()
                        for k in range(KD):
                            mm(pb[:, 0:n], wg3[:, k, dd * 128:(dd + 1) * 128], hT[:, k, c0:c0 + n], k == 0, k == KD - 1,
                               [wgk, hk(tgi)], [pk])
                        act(ug[u][:, dd, 0:n], pb[:, 0:n], AF.Gelu, [pk], [f"ug{u}"])
                    nch = 1 if samp else 4
                    np_ = NS if samp else 128
                    ncol = NS if samp else 128
                    for cc in range(nch):
                        cs_ = c0 + cc * 128
                        pz, pzk = nextpb()
                        for k in range(KD):
                            mm(pz[0:np_, 0:256], hT[:, k, cs_:cs_ + ncol], wg3[:, k, 256:512], k == 0, k == KD - 1,
                               [wgk, hk(tgi)], [pzk])
                        act(sg4[u][0:np_, cc, :], pz[0:np_, 0:256], AF.Gelu, [pzk], [f"sg4{u}"])

                def sguB(g, tgi, u):
                    wg3, wgk, wo, wok = groups[g]
                    c0, n = tgs[tgi]
                    samp = (n == NS)
                    nch = 1 if samp else 4
                    np_ = NS if samp else 128
                    for cc in range(nch):
                        bnstats(bst4[u][0:np_, cc, :], sg4[u][0:np_, cc, :], [f"sg4{u}"], [f"bst4{u}"])
                    for cc in range(nch):
                        bnaggr(mv4[u][0:np_, cc, :], bst4[u][0:np_, cc, :], [f"bst4{u}"], [f"mv4{u}"])
                    rstd_pool(rsv4[u][0:np_, 0:nch], sdv4[u][0:np_, 0:nch], mv4[u][0:np_, 0:nch, 1], LN_EPS,
                              [f"mv4{u}"], f"sdv4{u}", [f"rsv4{u}"])
                    for cc in range(nch):
                        ts('dve', sg4[u][0:np_, cc, :], sg4[u][0:np_, cc, :], mv4[u][0:np_, cc, 0:1], rsv4[u][0:np_, cc:cc + 1],
                           ALU.subtract, ALU.mult, [f"sg4{u}", f"mv4{u}", f"rsv4{u}"], [f"sg4{u}"])
                    tt('dve', sg4[u][0:np_, 0:nch, :], sg4[u][0:np_, 0:nch, :],
                       slgb[0:np_, g * 256:(g + 1) * 256].rearrange("p (o n) -> p o n", o=1).to_broadcast([np_, nch, 256]),
                       ALU.mult, [f"sg4{u}", "slgb"], [f"sg4{u}"])
                    if samp:
                        tt('dve', sg4[u][0:NS, 0, :], sg4[u][0:NS, 0, :], slbb[0:NS, g * 256:(g + 1) * 256], ALU.add,
                           [f"sg4{u}", "slbb"], [f"sg4{u}"])
                        dma('sp', sgur[:, g * 256:(g + 1) * 256], sg4[u][0:NS, 0, :], [f"sg4{u}"], [], f'o_sg4{u}')
                        cp('dve', snb4[u][0:NS, 0, :], sg4[u][0:NS, 0, :], [f"sg4{u}"], [f"snb4{u}"])
                    else:
                        tt('dve', snb4[u][:, :, :], sg4[u][:, :, :],
                           slbb[:, g * 256:(g + 1) * 256].rearrange("p (o n) -> p o n", o=1).to_broadcast([128, 4, 256]),
                           ALU.add, [f"sg4{u}", "slbb"], [f"snb4{u}"])
                    for cc in range(nch):
                        lc = cc * 128
                        for dd in range(2):
                            plk = ("PO", dd)
                            pl = PO[:, dd * 512:(dd + 1) * 512]
                            if samp:
                                mm(pl[:, 0:NS], snb4[u][0:NS, 0, dd * 128:(dd + 1) * 128], Wdiag[:, g * 16:(g + 1) * 16],
                                   True, True, [f"snb4{u}", "Wdiag"], [plk])
                            else:
                                mm(pl[:, lc:lc + 128], snb4[u][:, cc, dd * 128:(dd + 1) * 128], WsT[:, g, :], True, True,
                                   [f"snb4{u}", "WsT"], [plk])
                    for dd in range(2):
                        if samp:
                            tt('dve', mb[:, 0:n], PO[:, dd * 512:dd * 512 + n], biasb[:, g, 0:1].to_broadcast([128, NS]),
                               ALU.add, [("PO", dd), "biasb"], ["mb"])
                        else:
                            tt('dve', mb[:, :].rearrange("p (c i) -> p c i", c=4),
                               PO[:, dd * 512:(dd + 1) * 512].rearrange("p (c i) -> p c i", c=4),
                               biasb[:, g:g + 1, :].to_broadcast([128, 4, 128]), ALU.add, [("PO", dd), "biasb"], ["mb"])
                        tt('dve', goT[u][:, dd, 0:n], mb[:, 0:n], ug[u][:, dd, 0:n], ALU.mult,
                           ["mb", f"ug{u}"], [f"goT{u}"])
                    out_proj(wo, wok, goT[u], [f"goT{u}"], tgi, c0, n)

                units = [(g, tgi) for g in range(4) for tgi in range(len(tgs))]
                load_group(0)
                prev = None
                for ui, (g, tgi) in enumerate(units):
                    sguA(g, tgi, ui % 2)
                    if prev is not None:
                        sguB(*prev)
                    if tgi == 0 and g + 1 < 4:
                        load_group(g + 1)
                    prev = (g, tgi, ui % 2)
                sguB(*prev)
                pre['ffn0'] = ffn_load(0, 0)
                dump(p * 4 + 0)
                S.barrier()

            with ExitStack() as ph:
                st['wide'] = True
                rmsnorm(VR['nf0'], ph, f"b{p}")
                ffn(0, ph, f"a{p}", pre0=pre.pop('ffn0', None), hook3=conv_load_ab)
                st['wide'] = False
                dump(p * 4 + 1)
                S.barrier()


            with ExitStack() as ph:
                def T(name, shape, dt=F32):
                    return ph.enter_context(nc.sbuf_tensor(f"sb_{name}_c{p}", list(shape), dt))
                xgx = T("xgx", [128, KD, 30 + P], BF16)
                sig = [T(f"sig{i}", [128, 512]) for i in range(2)]
                xgs = T("xgs", [128, KD, NS])
                xgtail = T("xgtail", [128, KD, 32])
                dg = [T(f"dg{i}", [128, 128], BF16) for i in range(4)]
                convT = T("convT", [128, KD, 512])
                convTb = hT.bitcast(F32)
                hkeys = [hk(t_) for t_ in range(3)]
                sqc = T("sqc", [128, KD, 512], BF16)
                cn = sqc
                mean = T("mean", [128, 512]); ex2 = T("ex2", [128, 512])
                u1 = [T(f"u1{i}", [128, 512]) for i in range(2)]
                onesF = T("onesF", [128, 128])
                memset(onesF[:, :], 1.0, ["onesF"])
                if last:
                    hs = [T("hs0", [NS, D])]
                    tl = T("tl", [32, D])

                st['wide'] = True
                rmsnorm(VR['nm1'], ph, f"c{p}", ext=(sqc, mean, ex2, ["sqc"] + [("cn", k) for k in range(KD)], "mean", "ex2"))
                cp('dve', xgx[:, :, 0:30], hist[:, :, :], ["hist"], [("xgx", k, -1) for k in range(KD)])
                if 'conv' not in pre:
                    conv_load_ab()
                wa3, wak, wb3, wbk = pre.pop('conv')
                wc, wck, wcc = nextwb()
                wc3 = wc[:, :].rearrange("p (k c) -> p k c", k=KD)
                for hh in range(2):
                    wload(wc3[:, hh * 4:hh * 4 + 4, :], cwout.rearrange("(k p) c -> p k c", p=128)[:, hh * 4:hh * 4 + 4, :], wck, wcc)
                for tgi, (c0, n) in enumerate(tgs):
                    samp = (n == NS)
                    for cc in range(KD):
                        i = cc % 2
                        pa, pak = nextpb()
                        for k in range(KD):
                            mm(pa[:, 0:n], wa3[:, k, cc * 128:(cc + 1) * 128], hT[:, k, c0:c0 + n], k == 0, k == KD - 1,
                               [wak, hk(tgi)], [pak])
                        pg, pgk = nextpb()
                        for k in range(KD):
                            mm(pg[:, 0:n], wb3[:, k, cc * 128:(cc + 1) * 128], hT[:, k, c0:c0 + n], k == 0, k == KD - 1,
                               [wbk, hk(tgi)], [pgk])
                        act(sig[i][:, 0:n], pg[:, 0:n], AF.Sigmoid, [pgk, "vecT"], [f"sig{i}"],
                            bias=vecT[:, cc, VR['cbg']:VR['cbg'] + 1], scale=1.0)
                        if samp:
                            stt(xgs[:, cc, :], pa[:, 0:n], vecT[:, cc, VR['cba']:VR['cba'] + 1], sig[i][:, 0:n],
                                ALU.add, ALU.mult, [pak, "vecT", f"sig{i}"], [("xgs", cc)])
                        else:
                            stt(xgx[:, cc, 30 + c0:30 + c0 + n], pa[:, 0:n], vecT[:, cc, VR['cba']:VR['cba'] + 1],
                                sig[i][:, 0:n], ALU.add, ALU.mult, [pak, "vecT", f"sig{i}"], [("xgx", cc, tgi)])
                            if last and tgi == 1:
                                stt(xgtail[:, cc, :], pa[:, n - 32:n], vecT[:, cc, VR['cba']:VR['cba'] + 1],
                                    sig[i][:, n - 32:n], ALU.add, ALU.mult, [pak, "vecT", f"sig{i}"], [("xgt", cc)])
                if not last:
                    cp('dve', hist[:, :, :], xgx[:, :, P:P + 30], [("xgx", k, 1) for k in range(KD)], ["hist"])
                pre['ffn1'] = ffn_load(1, 0)
                ptg = [(tgi, c0, n) for tgi, (c0, n) in enumerate(tgs) if n != NS]
                convT2 = [convT, convTb]
                for cc in range(KD):
                    banks = [nextpb() for _ in ptg]
                    for j in range(CW):
                        di = st.setdefault('dg', 0) % 4
                        st['dg'] += 1
                        ts('dve', dg[di][:, :], identb[:, :], vecT[:, cc, VR['dww'] + j:VR['dww'] + j + 1], None,
                           ALU.mult, None, ["identb", "vecT"], [f"dg{di}"])
                        for (tgi, c0, n), (pb, pk) in zip(ptg, banks):
                            mm(pb[:, 0:n], dg[di][:, :], xgx[:, cc, c0 + j:c0 + j + n], j == 0, j == CW - 1,
                               [f"dg{di}", ("xgx", cc, tgi), ("xgx", cc, tgi - 1)], [pk])
                    for (tgi, c0, n), (pb, pk) in zip(ptg, banks):
                        act(convT2[tgi][:, cc, 0:n], pb[:, 0:n], AF.Identity, [pk, "vecT"],
                            [("convT", tgi % 2, cc)] + (hkeys if tgi % 2 else []),
                            bias=vecT[:, cc, VR['dwb']:VR['dwb'] + 1], scale=1.0)
                for tgi, (c0, n) in enumerate(tgs):
                    samp = (n == NS)
                    convT = convT2[tgi % 2]
                    if not samp:
                        pass
                    else:
                        for cc in range(KD):
                            stt(u1[0][:, cc * NS:(cc + 1) * NS], xgs[:, cc, :], vecT[:, cc, VR['dww'] + 30:VR['dww'] + 31],
                                hconv[:, cc * NS:(cc + 1) * NS], ALU.mult, ALU.add, [("xgs", cc), "vecT", "hconv"], ["u10"])
                            ts('dve', convT[:, cc, 0:n], u1[0][:, cc * NS:(cc + 1) * NS], vecT[:, cc, VR['dwb']:VR['dwb'] + 1],
                               None, ALU.add, None, ["u10", "vecT"], [("convT", tgi % 2, cc)])
                    ck = [("convT", tgi % 2, cc) for cc in range(KD)] + (hkeys if tgi % 2 else [])
                    act(sqc[:, :, 0:n], convT[:, :, 0:n], AF.Square, ck, ["sqc"] + [("cn", k) for k in range(KD)])
                    pm, pmk = nextpb()
                    for k in range(KD):
                        mm(pm[:, 0:n], onesF[:, :], convT[:, k, 0:n], k == 0, k == KD - 1, ck + ["onesF"], [pmk])
                    pq, pqk = nextpb()
                    for k in range(KD):
                        mm(pq[:, 0:n], onesb[:, :], sqc[:, k, 0:n], k == 0, k == KD - 1, ["sqc", "onesb"], [pqk])
                    ts('dve', mean[:, 0:n], pm[:, 0:n], 1.0 / D, None, ALU.mult, None, [pmk], ["mean"])
                    ts('dve', ex2[:, 0:n], pq[:, 0:n], 1.0 / D, None, ALU.mult, None, [pqk], ["ex2"])
                    tt('dve', u1[0][:, 0:n], mean[:, 0:n], mean[:, 0:n], ALU.mult, ["mean"], ["u10"])
                    tt('dve', ex2[:, 0:n], ex2[:, 0:n], u1[0][:, 0:n], ALU.subtract, ["ex2", "u10"], ["ex2"])
                    act(ex2[:, 0:n], ex2[:, 0:n], AF.Sqrt, ["ex2"], ["ex2"], bias=float(LN_EPS), scale=1.0)
                    recip(ex2[:, 0:n], ex2[:, 0:n], ["ex2"], ["ex2"])
                    tt('dve', mean[:, 0:n], mean[:, 0:n], ex2[:, 0:n], ALU.mult, ["mean", "ex2"], ["mean"])
                    for cc in range(KD):
                        i = cc % 2
                        tt('dve', u1[i][:, 0:n], convT[:, cc, 0:n], ex2[:, 0:n], ALU.mult, [("convT", tgi % 2, cc), "ex2"] + (hkeys if tgi % 2 else []), [f"u1{i}"])
                        tt('dve', u1[i][:, 0:n], u1[i][:, 0:n], mean[:, 0:n], ALU.subtract, [f"u1{i}", "mean"], [f"u1{i}"])
                        act(cn[:, cc, 0:n], u1[i][:, 0:n], AF.Silu, [f"u1{i}", "vecT"], [("cn", cc), "sqc"],
                            bias=vecT[:, cc, VR['clb']:VR['clb'] + 1], scale=vecT[:, cc, VR['clg']:VR['clg'] + 1])
                    for d in range(KD):
                        pb, pk = nextpb()
                        for k in range(KD):
                            mm(pb[:, 0:n], wc3[:, k, d * 128:(d + 1) * 128], cn[:, k, 0:n], k == 0, k == KD - 1,
                               [wck, ("cn", k)], [pk])
                        stt(xT[:, d, c0:c0 + n], pb[:, 0:n], vecT[:, d, VR['cbo']:VR['cbo'] + 1], xT[:, d, c0:c0 + n],
                            ALU.add, ALU.add, [pk, "vecT", xk(d, tgi)], [xk(d, tgi)])
                if last:
                    for half in range(2):
                        pb, pk = nextpb()
                        for kk in range(4):
                            k = half * 4 + kk
                            tp(pb[0:32, kk * 128:(kk + 1) * 128], xgtail[:, k, :], identf[:, :], [("xgt", k), "identf"], [pk])
                        cp('act', tl[:, half * 512:(half + 1) * 512], pb[0:32, :], [pk], [f"tl{half}"])
                    dma('sp', csp[:, :], tl[2:32, :], ["tl0", "tl1"], [], 'o_tl')
                    for half in range(2):
                        pb, pk = nextpb()
                        for kk in range(4):
                            k = half * 4 + kk
                            tp(pb[0:NS, kk * 128:(kk + 1) * 128], xgs[:, k, :], identf[:, :], [("xgs", k), "identf"], [pk])
                        cp('act', hs[0][0:NS, half * 512:(half + 1) * 512], pb[0:NS, :], [pk], [f"hs0"])
                    dma('sp', css.rearrange("(b j) d -> b j d", j=30)[:, 29, :], hs[0][0:NS, :], ["hs0"], [], 'o_hs0')
                dump(p * 4 + 2)
                st['wide'] = False
                S.barrier()

            with ExitStack() as ph:
                st['wide'] = True
                rmsnorm(VR['nf1'], ph, f"d{p}")
                ffn(1, ph, f"b{p}", pre0=pre.pop('ffn1', None))
                st['wide'] = False
                dump(p * 4 + 3)
                S.barrier()

            with ExitStack() as ph:
                sq = ph.enter_context(nc.sbuf_tensor(f"sqf{p}", [128, KD, 512], BF16))
                sd = ph.enter_context(nc.sbuf_tensor(f"sdf{p}", [128, 512], F32))
                rs = ph.enter_context(nc.sbuf_tensor(f"rsf{p}", [128, 512], F32))
                yT = ph.enter_context(nc.sbuf_tensor(f"yT{p}", [128, KD, 512], F32))
                yo = [ph.enter_context(nc.sbuf_tensor(f"yo{p}{i}", [128, D], F32)) for i in range(2)]
                xin_next = load_x_pre(p + 1, ph) if p + 1 < NPASS else None
                st['wide'] = True
                for tgi, (c0, n) in enumerate(tgs):
                    act(sq[:, :, 0:n], xT[:, :, c0:c0 + n], AF.Square, [xk(k, tgi) for k in range(KD)], ["sq"])
                    pb, pk = nextpb()
                    for k in range(KD):
                        mm(pb[:, 0:n], onesb[:, :], sq[:, k, 0:n], k == 0, k == KD - 1, ["sq", "onesb"], [pk])
                    act(sd[:, 0:n], pb[:, 0:n], AF.Sqrt, [pk], ["sd"], bias=float(RMS_EPS), scale=1.0 / D)
                    recip(rs[:, 0:n], sd[:, 0:n], ["sd"], ["rs"])
                    for k in range(KD):
                        stt(yT[:, k, 0:n], xT[:, k, c0:c0 + n], vecT[:, k, VR['fin']:VR['fin'] + 1], rs[:, 0:n],
                            ALU.mult, ALU.mult, [xk(k, tgi), "vecT", "rs"], [("yT", k)])
                    nch = 1 if n == NS else 4
                    np_ = NS if n == NS else 128
                    for cc in range(nch):
                        oi = st.setdefault('yo', 0) % 2
                        st['yo'] += 1
                        for half in range(2):
                            pb, pk = nextpb()
                            for kk in range(4):
                                k = half * 4 + kk
                                tp(pb[0:np_, kk * 128:(kk + 1) * 128], yT[:, k, cc * 128:cc * 128 + np_], identf[:, :],
                                   [("yT", k), "identf"], [pk])
                            cp('act' if half == 0 else 'dve', yo[oi][0:np_, half * 512:(half + 1) * 512], pb[0:np_, :],
                               [pk], [f"yo{oi}{half}"])
                        if n == NS:
                            dma('sp', ys[:, :], yo[oi][0:NS, :], [f"yo{oi}0", f"yo{oi}1"], [], f'o_yo{oi}')
                        else:
                            r0 = p * P + c0 + cc * 128
                            dma('sp', yp[r0:r0 + 128, :], yo[oi][:, :], [f"yo{oi}0", f"yo{oi}1"], [], f'o_yo{oi}')
                if p + 1 < NPASS:
                    load_x(p + 1, ph, xin_next)
                st['wide'] = False
                S.barrier()

        chan_sems = {c: es.enter_context(nc.semaphore("c_" + c)) for c in sorted(S.chan_n)}
        print("ops", len(S.ops), "chans", len(chan_sems), flush=True)
        with nc.Block() as block:
            S.emit(nc, block, sems, chan_sems, wait_all_chans={'const'},
                   final_chans=[c for c in sorted(S.chan_n) if c.startswith('o_')])
    return nc, consts


_CACHE = {}


def kernel(x_prompt, x_sample, state_ret, state_conv, norm_mix_g, norm_ffn_g, final_norm_g,
           ab_w_in, ab_w_out, ret_gn_g, sgu_ln_g, sgu_ln_b, sgu_w, sgu_b,
           conv_w_in, conv_b_in, conv_dw_w, conv_dw_b, conv_ln_g, conv_ln_b,
           conv_w_out, conv_b_out, ffn_w1, ffn_w2):
    f = lambda a: np.ascontiguousarray(np.asarray(a, dtype=np.float32))
    if 'nc' not in _CACHE:
        _CACHE['nc'] = build_nc()
    nc, consts = _CACHE['nc']
    vecs = np.concatenate([
        f(norm_mix_g)[0:1], f(norm_mix_g)[1:2], f(norm_ffn_g)[0:1], f(norm_ffn_g)[1:2], f(final_norm_g)[None, :],
        f(conv_b_in)[0:1, 0:1024], f(conv_b_in)[0:1, 1024:2048], f(conv_dw_b)[0:1], f(conv_ln_g)[0:1],
        f(conv_ln_b)[0:1], f(conv_b_out)[0:1], f(conv_dw_w)[0]], axis=0)
    assert vecs.shape == (NVEC, D)
    shared = dict(
        win=f(ab_w_in)[0], wout=f(ab_w_out)[0], gng=f(ret_gn_g), slg=f(sgu_ln_g), slb=f(sgu_ln_b),
        sguw=f(sgu_w)[0], sgub=f(sgu_b)[0].reshape(1, 512), cwin=f(conv_w_in)[0], cwout=f(conv_w_out)[0],
        w1=f(ffn_w1), w2=f(ffn_w2), vecs=f(vecs),
        dmask=consts['dmask'].reshape(128, 512), zeta=consts['zeta'], xitab=consts['xitab'].reshape(128, 512),
        ropec=consts['ropec'], ropes=consts['ropes'], ident=consts['ident'], tri=consts['tri'], ind=consts['ind'])
    xp_ = f(x_prompt); xs_ = f(x_sample); sr = f(state_ret); scv = f(state_conv)
    in_maps = []
    for c in range(8):
        m = dict(shared)
        m['xp'] = xp_[c]
        m['xs'] = xs_[c * NS:(c + 1) * NS, 0, :]
        m['sret'] = sr[0, c * NS:(c + 1) * NS]
        m['sconv'] = scv[0, c * NS:(c + 1) * NS].reshape(NS * 30, D)
        in_maps.append(m)
    res = run_bass_kernel_spmd(nc, in_maps, core_ids=list(range(8)))
    R = res.results
    _CACHE['R'] = R
    y_prompt = np.stack([R[c]['yp'] for c in range(8)], axis=0).astype(np.float32)
    y_sample = np.concatenate([R[c]['ys'] for c in range(8)], axis=0).reshape(128, 1, D).astype(np.float32)
    rsp = np.stack([R[c]['rsp'] for c in range(8)], axis=0)[None].astype(np.float32)
    rss = np.concatenate([R[c]['rss'] for c in range(8)], axis=0)[None].astype(np.float32)
    sgur = np.concatenate([R[c]['sgur'] for c in range(8)], axis=0).reshape(1, 128, 1, D).astype(np.float32)
    csp = np.stack([R[c]['csp'] for c in range(8)], axis=0)[None].astype(np.float32)
    css = np.concatenate([R[c]['css'].reshape(NS, 30, D) for c in range(8)], axis=0)[None].astype(np.float32)
    return (y_prompt, y_sample, rsp, rss, sgur, csp, css)
```
